# Optimizing a Trainium2 kernel written in Bass

```python
import jax, jax.numpy as jnp
from jax import lax
import numpy as np

D_MODEL = 2048
BATCH = 16
SEQ = 2048
DEPTH = 4

CTX_LEN = 256
GRID_W = 64
N_MIXERS = 2
D_FF = ((8 * D_MODEL + 3 * 256 - 1) // (3 * 256)) * 256
CONV_WIDTH = 31
N_HEADS_M = max(4, D_MODEL // 512)
DQK = D_MODEL // 2
DV = D_MODEL
HEAD_QK = DQK // N_HEADS_M
HEAD_V = DV // N_HEADS_M
CHUNK = 64
GATE_CAP = 15.0
EPS = 1e-6
N_CONV_LAYERS = (DEPTH + 1) // 2
N_MLSTM_LAYERS = DEPTH // 2
W_IN_COLS = 2 * DQK + 2 * DV + 4 * N_HEADS_M

kernel_name = "hybrid_conv_mlstm_prefix_dit"


def rmsnorm(x, g):
    xf = x.astype(jnp.float32)
    y = xf * lax.rsqrt(jnp.mean(xf * xf, axis=-1, keepdims=True) + EPS)
    return (y * g.astype(jnp.float32)).astype(x.dtype)


def layernorm(x, g, b):
    xf = x.astype(jnp.float32)
    mu = jnp.mean(xf, axis=-1, keepdims=True)
    var = jnp.mean(jnp.square(xf - mu), axis=-1, keepdims=True)
    y = (xf - mu) * lax.rsqrt(var + EPS)
    return (y * g.astype(jnp.float32) + b.astype(jnp.float32)).astype(x.dtype)


def modulation(cvec, w, b):
    m = jax.nn.silu(cvec) @ w + b
    parts = jnp.split(m, 6, axis=-1)
    if cvec.ndim == 2:
        parts = [p[:, None, :] for p in parts]
    return parts


def modulate(h, shift, scale):
    return h * (1 + scale) + shift


def swiglu_ffn(h, w_gu, w_down):
    g, u = jnp.split(h @ w_gu, 2, axis=-1)
    return (jax.nn.silu(g) * u) @ w_down


def depthwise_conv(x, w, b):
    k = w.shape[0]
    out = lax.conv_general_dilated(
        x, w[:, None, :].astype(x.dtype), window_strides=(1,),
        padding=[(k // 2, k // 2)], dimension_numbers=('NWC', 'WIO', 'NWC'),
        feature_group_count=x.shape[-1])
    return out + b


def conformer_conv(h, on_grid, w_pw1, b_pw1, w_dw, b_dw, ln_g, ln_b, w_pw2, b_pw2):
    B, L, _ = h.shape
    a = h @ w_pw1 + b_pw1
    d = a.shape[-1] // 2
    u = a[..., :d] * jax.nn.sigmoid(a[..., d:])
    if on_grid:
        rows = L // GRID_W
        half = d // 2
        ur = u[..., :half].reshape(B * rows, GRID_W, half)
        ur = depthwise_conv(ur, w_dw[:, :half], b_dw[:half]).reshape(B, L, half)
        uc = u[..., half:].reshape(B, rows, GRID_W, d - half).transpose(0, 2, 1, 3)
        uc = uc.reshape(B * GRID_W, rows, d - half)
        uc = depthwise_conv(uc, w_dw[:, half:], b_dw[half:])
        uc = uc.reshape(B, GRID_W, rows, d - half).transpose(0, 2, 1, 3).reshape(B, L, d - half)
        u = jnp.concatenate([ur, uc], axis=-1)
    else:
        u = depthwise_conv(u, w_dw, b_dw)
    u = layernorm(u, ln_g, ln_b)
    return jax.nn.silu(u) @ w_pw2 + b_pw2


def to_heads(a, d):
    B, L, _ = a.shape
    return a.reshape(B, L, -1, d).transpose(0, 2, 1, 3).astype(jnp.float32)


def mlstm_project(h, w_in, b_gates):
    B, L, _ = h.shape
    proj = h @ w_in
    q, k, v, o, g = jnp.split(proj, [DQK, 2 * DQK, 2 * DQK + DV, 2 * DQK + 2 * DV], axis=-1)
    q = to_heads(q, HEAD_QK)
    k = to_heads(k, HEAD_QK) * (HEAD_QK ** -0.5)
    v = to_heads(v, HEAD_V)
    g = g.astype(jnp.float32) + b_gates.astype(jnp.float32)
    g = GATE_CAP * jnp.tanh(g / GATE_CAP)
    g = g.reshape(B, L, 4, N_HEADS_M).transpose(2, 0, 3, 1)
    fwd = (g[0], jax.nn.log_sigmoid(g[1]))
    bwd = (g[2], jax.nn.log_sigmoid(g[3]))
    return q, k, v, o, fwd, bwd


def mlstm_scan(q, k, v, li, lf, state):
    B, H, L, _ = q.shape
    nc = L // CHUNK
    tril = jnp.tril(jnp.ones((CHUNK, CHUNK), dtype=bool))

    def chunks(a):
        a = a.reshape(a.shape[:2] + (nc, CHUNK) + a.shape[3:])
        return jnp.moveaxis(a, 2, 0)

    def step(carry, inp):
        C, n, m = carry
        qc, kc, vc, lic, lfc = inp
        b = jnp.cumsum(lfc, axis=-1)
        a = b + m[..., None]
        dmat = b[..., :, None] - b[..., None, :] + lic[..., None, :]
        dmat = jnp.where(tril, dmat, -jnp.inf)
        mt = jnp.maximum(a, jnp.max(dmat, axis=-1))
        w_inter = jnp.exp(a - mt)
        s = jnp.einsum('bhtd,bhsd->bhts', qc, kc) * jnp.exp(dmat - mt[..., None])
        num = (w_inter[..., None] * jnp.einsum('bhtd,bhde->bhte', qc, C)
               + jnp.einsum('bhts,bhse->bhte', s, vc))
        den = w_inter * jnp.einsum('bhtd,bhd->bht', qc, n) + jnp.sum(s, axis=-1)
        h = num / jnp.maximum(jnp.abs(den), jnp.exp(-mt))[..., None]
        bT = b[..., -1]
        gl = bT[..., None] - b + lic
        m_new = jnp.maximum(bT + m, jnp.max(gl, axis=-1))
        decay = jnp.exp(bT + m - m_new)
        kw = kc * jnp.exp(gl - m_new[..., None])[..., None]
        C_new = decay[..., None, None] * C + jnp.einsum('bhsd,bhse->bhde', kw, vc)
        n_new = decay[..., None] * n + jnp.sum(kw, axis=2)
        return (C_new, n_new, m_new), h

    state, hs = lax.scan(step, state, (chunks(q), chunks(k), chunks(v), chunks(li), chunks(lf)))
    h = jnp.moveaxis(hs, 0, 2).reshape(B, H, L, v.shape[-1])
    return h, state


def zero_state(B):
    return (jnp.zeros((B, N_HEADS_M, HEAD_QK, HEAD_V), jnp.float32),
            jnp.zeros((B, N_HEADS_M, HEAD_QK), jnp.float32),
            jnp.zeros((B, N_HEADS_M), jnp.float32))


def mlstm_bidir(q, k, v, fwd, bwd, init_f, init_b):
    h_f, st_f = mlstm_scan(q, k, v, fwd[0], fwd[1], init_f)
    flip = lambda a: jnp.flip(a, axis=2)
    h_b, st_b = mlstm_scan(flip(q), flip(k), flip(v), flip(bwd[0]), flip(bwd[1]), init_b)
    return h_f + flip(h_b), st_f, st_b


def mlstm_output(hs, o, hn_g, w_out):
    B, H, L, dv = hs.shape
    hn = hs * lax.rsqrt(jnp.mean(hs * hs, axis=-1, keepdims=True) + EPS)
    hn = hn.transpose(0, 2, 1, 3).reshape(B, L, H * dv).astype(o.dtype)
    return (hn * hn_g * jax.nn.sigmoid(o)) @ w_out


def setup_inputs(seed: int = 0) -> dict:
    key = jax.random.key(seed)
    ks = jax.random.split(key, 24)
    f32 = jnp.float32

    def nrm(k, shape, scale):
        return jax.random.normal(k, shape, f32) * scale

    D, F, H = D_MODEL, D_FF, N_HEADS_M
    gate_base = jnp.tile(jnp.concatenate([jnp.zeros((H,), f32), jnp.full((H,), 3.0, f32)]), 2)
    return {
        "x": nrm(ks[0], (BATCH, SEQ, D), 1.0),
        "c": nrm(ks[1], (BATCH, D), 1.0),
        "ctx": nrm(ks[2], (BATCH, CTX_LEN, D), 1.0),
        "c_ctx": nrm(ks[3], (D,), 1.0),
        "norm1_g": 1.0 + nrm(ks[4], (DEPTH, D), 0.05),
        "norm2_g": 1.0 + nrm(ks[5], (DEPTH, D), 0.05),
        "w_mod": nrm(ks[6], (DEPTH, D, 6 * D), 0.5 * D ** -0.5),
        "b_mod": nrm(ks[7], (DEPTH, 6 * D), 0.02),
        "w_gu": nrm(ks[8], (DEPTH, D, 2 * F), D ** -0.5),
        "w_down": nrm(ks[9], (DEPTH, F, D), F ** -0.5),
        "conv_w_pw1": nrm(ks[10], (N_CONV_LAYERS, D, 2 * D), D ** -0.5),
        "conv_b_pw1": nrm(ks[11], (N_CONV_LAYERS, 2 * D), 0.02),
        "conv_w_dw": nrm(ks[12], (N_CONV_LAYERS, CONV_WIDTH, D), CONV_WIDTH ** -0.5),
        "conv_b_dw": nrm(ks[13], (N_CONV_LAYERS, D), 0.02),
        "conv_ln_g": 1.0 + nrm(ks[14], (N_CONV_LAYERS, D), 0.05),
        "conv_ln_b": nrm(ks[15], (N_CONV_LAYERS, D), 0.02),
        "conv_w_pw2": nrm(ks[16], (N_CONV_LAYERS, D, D), D ** -0.5),
        "conv_b_pw2": nrm(ks[17], (N_CONV_LAYERS, D), 0.02),
        "m_w_in": nrm(ks[18], (N_MLSTM_LAYERS, D, W_IN_COLS), D ** -0.5),
        "m_b_gates": gate_base[None, :] + nrm(ks[19], (N_MLSTM_LAYERS, 4 * H), 0.1),
        "m_hn_g": 1.0 + nrm(ks[20], (N_MLSTM_LAYERS, D), 0.05),
        "m_w_out": nrm(ks[21], (N_MLSTM_LAYERS, D, D), D ** -0.5),
        "final_g": 1.0 + nrm(ks[22], (D,), 0.05),
    }


def reference(x, c, ctx, c_ctx, norm1_g, norm2_g, w_mod, b_mod, w_gu, w_down,
              conv_w_pw1, conv_b_pw1, conv_w_dw, conv_b_dw, conv_ln_g, conv_ln_b,
              conv_w_pw2, conv_b_pw2, m_w_in, m_b_gates, m_hn_g, m_w_out, final_g):
    B = x.shape[0]
    for i in range(DEPTH):
        last = i == DEPTH - 1
        j = i // N_MIXERS
        sx1, cx1, gx1, sx2, cx2, gx2 = modulation(c, w_mod[i], b_mod[i])
        sc1, cc1, gc1, sc2, cc2, gc2 = modulation(c_ctx, w_mod[i], b_mod[i])
        hx = modulate(rmsnorm(x, norm1_g[i]), sx1, cx1)
        hc = modulate(rmsnorm(ctx, norm1_g[i]), sc1, cc1)
        if i % N_MIXERS == 0:
            conv_p = (conv_w_pw1[j], conv_b_pw1[j], conv_w_dw[j], conv_b_dw[j],
                      conv_ln_g[j], conv_ln_b[j], conv_w_pw2[j], conv_b_pw2[j])
            yx = conformer_conv(hx, True, *conv_p)
            if not last:
                yc = conformer_conv(hc, False, *conv_p)
        else:
            qc, kc, vc, oc, fc, bc = mlstm_project(hc, m_w_in[j], m_b_gates[j])
            hsc, st_f, st_b = mlstm_bidir(qc, kc, vc, fc, bc, zero_state(B), zero_state(B))
            qx, kx, vx, ox, fx, bx = mlstm_project(hx, m_w_in[j], m_b_gates[j])
            hsx, _, _ = mlstm_bidir(qx, kx, vx, fx, bx, st_f, st_b)
            yx = mlstm_output(hsx, ox, m_hn_g[j], m_w_out[j])
            if not last:
                yc = mlstm_output(hsc, oc, m_hn_g[j], m_w_out[j])
        x = x + gx1 * yx
        x = x + gx2 * swiglu_ffn(modulate(rmsnorm(x, norm2_g[i]), sx2, cx2), w_gu[i], w_down[i])
        if not last:
            ctx = ctx + gc1 * yc
            ctx = ctx + gc2 * swiglu_ffn(modulate(rmsnorm(ctx, norm2_g[i]), sc2, cc2), w_gu[i], w_down[i])
    return rmsnorm(x, final_g)
```

```python
import contextlib
import numpy as np
import concourse.bass as bass
import concourse.mybir as mybir
from concourse.bass_utils import run_bass_kernel_spmd

F32 = mybir.dt.float32
BF16 = mybir.dt.bfloat16
AF = mybir.ActivationFunctionType
ALU = mybir.AluOpType

D = 2048
FF = 5632
TT = 4608
NBLK = 9
KC = 16
FC = 44
EPS = 1e-6
NEG = -30000.0
STRICT = True
MOD_OVERLAP = True


def grp_of_blk(blk):
    return 2 if blk == 0 else (0 if blk <= 4 else 1)


class Sem:
    def __init__(self, h):
        self.h = h
        self.n = 0


class Buf:
    __slots__ = ("ready", "free", "dsem")

    def __init__(self):
        self.ready = {}
        self.free = {}
        self.dsem = None


class Prog:
    ENG = ("sync", "act", "dve", "pool", "pe")

    def __init__(self, nc, n_dsem=48, strict=True):
        self.nc = nc
        self.strict = strict
        self.gstack = contextlib.ExitStack()
        self.nsem = 0
        self.nt = 0
        self.esem = {}
        self.dpool = [self._sem("d") for _ in range(n_dsem)]
        self.dspool = [self._sem("ds") for _ in range(6)]
        self.bar = self._sem("bar")
        self.bar_t = self.gstack.enter_context(nc.sbuf_tensor("bar_t", [128, 8], F32))
        self.new_epoch()
        self._begin()
        self.first = True

    def _sem(self, name):
        h = self.gstack.enter_context(self.nc.semaphore(f"s{self.nsem}_{name}"))
        self.nsem += 1
        return Sem(h)

    def new_epoch(self):
        for e in ("act", "dve", "pool", "pe"):
            self.esem[e] = self._sem(e)

    def _begin(self):
        self.q = {e: [] for e in self.ENG}
        self.pstack = contextlib.ExitStack()
        self.di = 0
        self.dsi = 0
        self.used_d = []

    def tile(self, name, shape, dt):
        self.nt += 1
        return self.pstack.enter_context(self.nc.sbuf_tensor(f"{name}_{self.nt}", list(shape), dt))

    def gtile(self, name, shape, dt):
        self.nt += 1
        return self.gstack.enter_context(self.nc.sbuf_tensor(f"{name}_{self.nt}", list(shape), dt))

    def psum(self, name, shape=(128, 512), dt=F32):
        self.nt += 1
        return self.pstack.enter_context(self.nc.psum_tensor(f"{name}_{self.nt}", list(shape), dt))

    def _dsem(self, buf, eng):
        if buf.dsem is None:
            if eng == "pool":
                assert self.dsi < len(self.dspool), "out of sw dma semaphores"
                buf.dsem = self.dspool[self.dsi]
                self.dsi += 1
            else:
                assert self.di < len(self.dpool), "out of dma semaphores"
                buf.dsem = self.dpool[self.di]
                self.di += 1
            self.used_d.append(buf.dsem)
        return buf.dsem

    def op(self, eng, fn, reads=(), writes=(), dma_buf=None, signal=True, marks=()):
        if dma_buf is not None:
            sem = self._dsem(dma_buf, eng)
            amt = 16
            own = None
        else:
            sem = self.esem[eng]
            amt = 1
            own = sem if (eng == "pe" or not self.strict) else None
        waits = {}
        for b in reads:
            for s, v in b.ready.items():
                if s is not own:
                    waits[s] = max(waits.get(s, 0), v)
        for b in writes:
            for s, v in list(b.ready.items()) + list(b.free.items()):
                if s is not own:
                    waits[s] = max(waits.get(s, 0), v)
        if signal:
            sem.n += amt
            val = sem.n
            inc = (sem, amt)
        else:
            val = sem.n + amt
            inc = None
        for b in reads:
            b.free[sem] = max(b.free.get(sem, 0), val)
        for b in writes:
            b.ready[sem] = max(b.ready.get(sem, 0), val)
        for b in marks:
            b.ready[sem] = max(b.ready.get(sem, 0), val)
        self.q[eng].append((fn, list(waits.items()), inc))

    def dma(self, eng, out, in_, reads=(), writes=(), sem_buf=None, marks=()):
        self.op(eng, lambda e: e.dma_start(out=out, in_=in_), reads=reads, writes=writes, dma_buf=sem_buf, marks=marks)

    def end_phase(self):
        waits = [(s, s.n) for s in self.esem.values() if s is not self.esem["dve"] and s.n > 0]
        waits += [(s, s.n) for s in self.used_d if s.n > 0]
        self.bar.n += 1
        bt = self.bar_t
        self.q["dve"].append((lambda e: e.memset(bt[0:1, 0:1], 0.0), waits, (self.bar, 1)))
        first = self.first
        barv = self.bar.n - 1
        qs = self.q
        with self.nc.Block() as block:
            def mk(items):
                def run(e):
                    seen = {}
                    if not first:
                        e.wait_ge(self.bar.h, barv)
                    for fn, waits, inc in items:
                        for s, v in waits:
                            if v <= 0 or seen.get(id(s), 0) >= v:
                                continue
                            e.wait_ge(s.h, v)
                            seen[id(s)] = v
                        ins = fn(e)
                        if inc is not None:
                            ins.then_inc(inc[0].h, inc[1])
                return run
            block.sync(mk(qs["sync"]))
            block.scalar(mk(qs["act"]))
            block.vector(mk(qs["dve"]))
            block.gpsimd(mk(qs["pool"]))
            block.tensor(mk(qs["pe"]))
        self.pstack.close()
        self.first = False
        self._begin()

    def finish(self):
        barv = self.bar.n
        with self.nc.Block() as block:
            block.sync(lambda e: e.wait_ge(self.bar.h, barv))
        self.gstack.close()


class Ring:
    def __init__(self, P, name, n, shape, dt):
        self.t = [P.tile(f"{name}{i}", shape, dt) for i in range(n)]
        self.b = [Buf() for _ in range(n)]
        self.i = 0
        self.n = n

    def next(self):
        k = self.i % self.n
        self.i += 1
        return self.t[k], self.b[k]


def build(depth=4, dbg=None, layers=None):
    nc = bass.Bass("TRN2", target_bir_lowering=False)

    def din(name, shape, dt=F32):
        return nc.dram_tensor(name, list(shape), dt, kind="ExternalInput").ap()

    def dscr(name, shape, dt):
        return nc.dram_tensor(name, list(shape), dt, kind="Internal").ap()

    xin = din("xin", [D, TT])
    cT = din("cT", [128, KC, 3])
    w_mod = din("w_mod", [4, 96, 128, D])
    b_mod = din("b_mod", [128, 4, 96])
    w_gu = din("w_gu", [4, 88, 128, D])
    w_down = din("w_down", [4, 16, 128, FF])
    n1g = din("n1g", [128, 4, KC])
    n2g = din("n2g", [128, 4, KC])
    fing = din("fing", [128, KC])
    c_pw1 = din("c_pw1", [2, 32, 128, D])
    c_bpw1 = din("c_bpw1", [128, 2, 32])
    c_wdw = din("c_wdw", [128, 2, KC, 31])
    c_bdw = din("c_bdw", [128, 2, KC])
    c_lng = din("c_lng", [128, 2, KC])
    c_lnb = din("c_lnb", [128, 2, KC])
    c_pw2 = din("c_pw2", [2, 16, 128, D])
    c_bpw2 = din("c_bpw2", [128, 2, KC])
    m_win = din("m_win", [2, 48, 128, D])
    m_wg = din("m_wg", [128, 2, KC, 16])
    m_bg = din("m_bg", [4, 2, 4])
    m_hng = din("m_hng", [128, 2, KC])
    m_wout = din("m_wout", [2, 16, 128, D])
    identd = din("ident", [128, 128])
    masksd = din("masks", [128, 8, 512])
    seld = din("sel", [4, 4, 128])

    out = nc.dram_tensor("out", [D, 4096], F32, kind="ExternalOutput").ap()

    XR = [dscr("xr0", [D, TT], F32), dscr("xr1", [D, TT], F32)]
    HB = dscr("hb", [D, TT], BF16)
    HB2 = dscr("hb2", [D, TT], BF16)
    UG = dscr("ug", [D, TT], F32)
    VC = dscr("vc", [D, TT], F32)
    HID = dscr("hid", [FF, TT], BF16)
    QD = dscr("qd", [1024, TT], BF16)
    KD = dscr("kd", [1024, TT], BF16)
    VD = dscr("vd", [TT, D], BF16)
    SO = dscr("so", [D, TT], BF16)
    GD = dscr("gd", [4, 4, TT], F32)

    P = Prog(nc, strict=STRICT)
    MOD = P.gtile("mod", [128, 4, 96, 3], F32)
    A1 = P.gtile("a1", [128, 4, KC, 3], F32)
    A2 = P.gtile("a2", [128, 4, KC, 3], F32)
    GB1 = P.gtile("gb1", [128, 4, KC, 3], F32)
    n1t = P.gtile("n1t", [128, 4, KC], F32)
    n2t = P.gtile("n2t", [128, 4, KC], F32)
    fint = P.gtile("fint", [128, KC], F32)
    bmt = P.gtile("bmt", [128, 4, 96], F32)
    cb1 = P.gtile("cb1", [128, 2, 32], F32)
    cbd = P.gtile("cbd", [128, 2, KC], F32)
    clg = P.gtile("clg", [128, 2, KC], F32)
    clb = P.gtile("clb", [128, 2, KC], F32)
    cb2 = P.gtile("cb2", [128, 2, KC], F32)
    mhg = P.gtile("mhg", [128, 2, KC], F32)
    mbg = P.gtile("mbg", [4, 2, 4], F32)
    ident = P.gtile("ident", [128, 128], F32)
    ones_b = P.gtile("ones_b", [128, 128], BF16)
    CONST = Buf()

    def MODv(l, part, kc, g):
        return MOD[:, l, part * 16 + kc, g:g + 1]

    csg = P.gtile("csg", [128, KC, 3], BF16)

    def mod_gen(l, wr, ps, pb):
        def issue(n4):
            wt, wb = wr.next()
            P.dma("pool", wt[:], w_mod[l, n4 * 4:(n4 + 1) * 4].rearrange("n p k -> p n k"), writes=[wb], sem_buf=wb)
            return wt, wb
        nxt_ = issue(0)
        for n4 in range(24):
            wt, wb = nxt_
            if n4 + 1 < 24:
                nxt_ = issue(n4 + 1)
            for ci in range(4):
                n = n4 * 4 + ci
                for kc in range(KC):
                    P.op("pe", lambda e, wt=wt, ci=ci, kc=kc, n=n: e.matmul(
                        ps[:, n * 3:(n + 1) * 3], lhsT=wt[:, ci, kc * 128:(kc + 1) * 128], rhs=csg[:, kc, :],
                        start=(kc == 0), stop=(kc == KC - 1)),
                        reads=[wb, CONST], writes=[pb], signal=(kc == KC - 1))
            yield
        for g in range(3):
            P.op("dve", lambda e, g=g: e.tensor_tensor(
                out=MOD[:, l, :, g], in0=ps[:, 0:288].rearrange("p (n g) -> p n g", g=3)[:, :, g], in1=bmt[:, l, :], op=ALU.add),
                reads=[pb, CONST], writes=[CONST])
        for g in range(3):
            P.op("dve", lambda e, g=g: e.scalar_tensor_tensor(
                out=A1[:, l, :, g], in0=MOD[:, l, 16:32, g], scalar=1.0, in1=n1t[:, l, :],
                op0=ALU.add, op1=ALU.mult), reads=[CONST], writes=[CONST])
            P.op("dve", lambda e, g=g: e.scalar_tensor_tensor(
                out=A2[:, l, :, g], in0=MOD[:, l, 64:80, g], scalar=1.0, in1=n2t[:, l, :],
                op0=ALU.add, op1=ALU.mult), reads=[CONST], writes=[CONST])
            if l % 2 == 0:
                P.op("dve", lambda e, g=g: e.tensor_tensor(
                    out=GB1[:, l, :, g], in0=MOD[:, l, 32:48, g], in1=cb2[:, l // 2, :], op=ALU.mult),
                    reads=[CONST], writes=[CONST])
        mod_done.add(l)
        yield

    mod_done = set()

    def mod_start(nslots=2):
        todo = [l for l in (layers if layers is not None else range(depth)) if l not in mod_done and l not in mod_started]
        if not todo:
            return None
        l = todo[0]
        mod_started.add(l)
        wr = Ring(P, "wm", nslots, [128, 4, D], BF16)
        return mod_gen(l, wr, P.psum("mps"), Buf())

    mod_started = set()

    def phase_mod():
        for t, d in ((n1t, n1g), (n2t, n2g), (fint, fing), (bmt, b_mod), (cb1, c_bpw1),
                     (cbd, c_bdw), (clg, c_lng), (clb, c_lnb), (cb2, c_bpw2), (mhg, m_hng), (mbg, m_bg),
                     (ident, identd)):
            P.dma("sync", t[:], d, writes=[CONST], sem_buf=CONST)
        P.op("dve", lambda e: e.memset(ones_b[:], 1.0), writes=[CONST])
        cf = P.tile("cf", [128, KC, 3], F32)
        cb = Buf()
        P.dma("sync", cf[:], cT, writes=[cb], sem_buf=cb)
        P.op("act", lambda e: e.activation(out=csg[:], in_=cf[:], func=AF.Silu), reads=[cb], writes=[CONST])
        first_layers = [l for l in (layers if layers is not None else range(depth))]
        npre = 1 if (first_layers and first_layers[0] % 2 == 0 and MOD_OVERLAP) else len(first_layers)
        for _ in range(npre):
            gen = mod_start(nslots=5)
            if gen is None:
                break
            for _ in gen:
                pass
        P.end_phase()

    def phase_norm(src, Asc, Bfn, dst, dst_dt, blks=range(NBLK), dst_tok_off=0):
        xts = [P.tile(f"nx{i}", [128, KC, 512], F32) for i in range(2)]
        xbs = [[Buf() for _ in range(KC)] for _ in range(2)]
        sq = P.tile("nsq", [128, KC, 512], BF16)
        sqb = [Buf(), Buf()]
        hr = Ring(P, "nh", 2, [128, KC, 512], dst_dt)
        ps = P.psum("nps")
        pb = Buf()
        rs = P.tile("nrs", [128, 512], F32)
        rb = Buf()
        srcv = src.rearrange("(k p) t -> p k t", p=128)
        dstv = dst.rearrange("(k p) t -> p k t", p=128)
        H = KC // 2
        blks = list(blks)
        P.dma("sync", xts[0][:], srcv[:, :, blks[0] * 512:blks[0] * 512 + 512], writes=xbs[0], sem_buf=xbs[0][0])
        for bi, blk in enumerate(blks):
            g = grp_of_blk(blk)
            t0 = blk * 512
            xt, xb = xts[bi % 2], xbs[bi % 2]
            ht, hb = hr.next()
            if bi + 1 < len(blks):
                nb = blks[bi + 1] * 512
                P.dma("sync", xts[(bi + 1) % 2][:], srcv[:, :, nb:nb + 512], writes=xbs[(bi + 1) % 2], sem_buf=xbs[(bi + 1) % 2][0])
            P.op("pool", lambda e, xt=xt: e.tensor_tensor(out=sq[:, 0:H, :], in0=xt[:, 0:H, :], in1=xt[:, 0:H, :], op=ALU.mult), reads=xb[0:H], writes=[sqb[0]])
            P.op("act", lambda e, xt=xt: e.activation(out=sq[:, H:KC, :], in_=xt[:, H:KC, :], func=AF.Square), reads=xb[H:KC], writes=[sqb[1]])
            for kc in range(KC):
                P.op("pe", lambda e, kc=kc: e.matmul(ps[:], lhsT=ones_b[:], rhs=sq[:, kc, :], start=(kc == 0), stop=(kc == KC - 1)),
                     reads=[sqb[kc // H], CONST], writes=[pb], signal=(kc == KC - 1))
            P.op("act", lambda e: e.activation(out=rs[:], in_=ps[:], func=AF.Sqrt, bias=epst[:, 0:1], scale=1.0 / D), reads=[pb, CONST], writes=[rb])
            P.op("dve", lambda e: e.reciprocal(out=rs[:], in_=rs[:]), reads=[rb], writes=[rb])
            for kc in range(KC):
                P.op("dve", lambda e, xt=xt, kc=kc, g=g: e.scalar_tensor_tensor(
                    out=xt[:, kc, :], in0=xt[:, kc, :], scalar=Asc(kc, g), in1=rs[:], op0=ALU.mult, op1=ALU.mult),
                    reads=[xb[kc], rb, CONST], writes=[xb[kc]])
                bb = Bfn(kc, g)
                if bb is None:
                    P.op("act", lambda e, xt=xt, ht=ht, kc=kc: e.activation(out=ht[:, kc, :], in_=xt[:, kc, :], func=AF.Identity),
                         reads=[xb[kc]], writes=[hb])
                else:
                    P.op("act", lambda e, xt=xt, ht=ht, kc=kc, bb=bb: e.activation(out=ht[:, kc, :], in_=xt[:, kc, :], func=AF.Identity, bias=bb, scale=1.0),
                         reads=[xb[kc], CONST], writes=[hb])
            P.dma("sync", dstv[:, :, t0 - dst_tok_off:t0 - dst_tok_off + 512], ht[:], reads=[hb], sem_buf=hb)
        P.end_phase()

    def grp_of_tok(t):
        return 2 if t < 512 else (0 if t < 2560 else 1)

    class NormConsumer:
        def __init__(self, src, Asc, Bfn, dst, dst_dt, dst_tok_off, T):
            self.src, self.Asc, self.Bfn, self.dst_dt, self.off, self.T = src, Asc, Bfn, dst_dt, dst_tok_off, T
            self.xt = P.tile("kx", [128, KC, T], F32)
            self.xb = [Buf() for _ in range(KC)]
            self.sq = P.tile("ksq", [128, KC, T], BF16)
            self.sqb = [Buf(), Buf()]
            self.inplace = (dst_dt == F32)
            self.ht = self.xt if self.inplace else P.tile("kh", [128, KC, T], dst_dt)
            self.hb = Buf()
            self.ps = P.psum("kps")
            self.pb = Buf()
            self.rs = P.tile("krs", [128, T], F32)
            self.rb = Buf()
            self.srcv = src.rearrange("(k p) t -> p k t", p=128)
            self.dstv = dst.rearrange("(k p) t -> p k t", p=128)
            self.q = []

        def add(self, tok0, ntok, dep):
            T, xt, xb, sq, sqb, ht, hb, ps, pb, rs, rb = self.T, self.xt, self.xb, self.sq, self.sqb, self.ht, self.hb, self.ps, self.pb, self.rs, self.rb
            H = KC // 2
            Asc, Bfn = self.Asc, self.Bfn
            for t0 in range(tok0, tok0 + ntok, T):
                g = grp_of_tok(t0)

                def s1(t0=t0):
                    P.dma("sync", xt[:], self.srcv[:, :, t0:t0 + T], reads=[dep], writes=xb, sem_buf=xb[0])

                def s2():
                    P.op("pool", lambda e: e.tensor_tensor(out=sq[:, 0:H, :], in0=xt[:, 0:H, :], in1=xt[:, 0:H, :], op=ALU.mult), reads=xb[0:H], writes=[sqb[0]])
                    P.op("act", lambda e: e.activation(out=sq[:, H:KC, :], in_=xt[:, H:KC, :], func=AF.Square), reads=xb[H:KC], writes=[sqb[1]])

                def s3():
                    for kc in range(KC):
                        P.op("pe", lambda e, kc=kc: e.matmul(ps[:, 0:T], lhsT=ones_b[:], rhs=sq[:, kc, :], start=(kc == 0), stop=(kc == KC - 1)),
                             reads=[sqb[kc // H], CONST], writes=[pb], signal=(kc == KC - 1))

                def s4():
                    P.op("act", lambda e: e.activation(out=rs[:], in_=ps[:, 0:T], func=AF.Sqrt, bias=epst[:, 0:1], scale=1.0 / D), reads=[pb, CONST], writes=[rb])
                    P.op("dve", lambda e: e.reciprocal(out=rs[:], in_=rs[:]), reads=[rb], writes=[rb])

                def s5(g=g, lo=0, hi=H):
                    for kc in range(lo, hi):
                        P.op("dve", lambda e, kc=kc: e.scalar_tensor_tensor(
                            out=xt[:, kc, :], in0=xt[:, kc, :], scalar=Asc(kc, g), in1=rs[:], op0=ALU.mult, op1=ALU.mult),
                            reads=[xb[kc], rb, CONST], writes=[xb[kc]])
                        bb = Bfn(kc, g)
                        if bb is None:
                            P.op("act", lambda e, kc=kc: e.activation(out=ht[:, kc, :], in_=xt[:, kc, :], func=AF.Identity),
                                 reads=[xb[kc]], writes=[hb])
                        else:
                            P.op("act", lambda e, kc=kc, bb=bb: e.activation(out=ht[:, kc, :], in_=xt[:, kc, :], func=AF.Identity, bias=bb, scale=1.0),
                                 reads=[xb[kc], CONST], writes=[hb])

                def s5b(g=g):
                    s5(g, H, KC)

                def s6(t0=t0):
                    P.dma("sync", self.dstv[:, :, t0 - self.off:t0 - self.off + T], ht[:], reads=([hb] + (list(xb) if self.inplace else [])), sem_buf=hb)
                nop = (lambda: None)
                self.q.extend([s1, nop, nop, s2, nop, s3, s4, nop, s5, s5b, s6])

        def pop(self):
            if self.q:
                self.q.pop(0)()

        def flush(self):
            while self.q:
                self.q.pop(0)()

    def gemm(src, kcn, sbs, units, wsrc, epi, umax, extra=None, nw=3, pre=None, nin=2, consumer=None, npb=2):
        sbmax = max(n for _, n in sbs) * 512
        intiles = [P.tile(f"gin{i}", [128, kcn, sbmax], BF16) for i in range(nin)]
        inbs = [Buf() for _ in range(nin)]
        wr = Ring(P, "gw", nw, [128, umax, kcn * 128], BF16)
        ps = [[P.psum("gps") for _ in range(umax if umax <= 2 else 1)] for _ in range(npb)]
        psb = [[Buf() for _ in p] for p in ps]
        gi = 0
        order = [(unit, t0_ + tb_ * 512) for (t0_, nb_) in sbs for unit in units for tb_ in range(nb_)]
        st = (pre(order) if getattr(pre, "wants_order", False) else pre()) if pre is not None else None
        cons = consumer() if consumer is not None else None
        sb_dep = [Buf() for _ in sbs]

        def load_in(i):
            tok0, nblk = sbs[i]
            it, ib = intiles[i % nin], inbs[i % nin]
            for kc in range(kcn):
                P.dma("sync", it[:, kc, 0:nblk * 512], src[kc * 128:(kc + 1) * 128, tok0:tok0 + nblk * 512], writes=[ib], sem_buf=ib)

        load_in(0)
        for sbi, (tok0, nblk) in enumerate(sbs):
            if sbi + 1 < len(sbs) and nin > 1:
                load_in(sbi + 1)
            intile, inb = intiles[sbi % nin], inbs[sbi % nin]
            if isinstance(st, dict):
                st["dep"] = sb_dep[sbi]
            for ui, unit in enumerate(units):
                wt, wb = wr.next()
                for ci, ch in enumerate(unit):
                    P.dma("pool", wt[:, ci, :], wsrc(ch), writes=[wb], sem_buf=wb)
                for tb in range(nblk):
                    s = gi % npb
                    gi += 1
                    for ci in range(len(unit)):
                        for kc in range(kcn):
                            P.op("pe", lambda e, p=ps[s][ci], wt=wt, ci=ci, kc=kc, tb=tb, intile=intile: e.matmul(
                                p[:], lhsT=wt[:, ci, kc * 128:(kc + 1) * 128], rhs=intile[:, kc, tb * 512:(tb + 1) * 512],
                                start=(kc == 0), stop=(kc == kcn - 1)),
                                reads=[wb, inb], writes=[psb[s][ci]], signal=(kc == kcn - 1))
                    epi(ui, unit, tok0 + tb * 512, [ps[s][ci] for ci in range(len(unit))], [psb[s][ci] for ci in range(len(unit))], st)
                    if cons is not None:
                        cons.pop()
            if extra is not None:
                extra(tok0, nblk, intile, inb, wr, ps, psb, st)
            if nin == 1 and sbi + 1 < len(sbs):
                load_in(sbi + 1)
            if cons is not None:
                cons.add(tok0, nblk * 512, sb_dep[sbi])
        if cons is not None:
            cons.flush()
        P.end_phase()

    SB16 = [(0, 3), (1536, 3), (3072, 3)]
    SB44 = [(0, 2), (1024, 2), (2048, 2), (3072, 2), (4096, 1)]
    SB16L = [(512, 3), (2048, 3), (3584, 2)]
    SB44L = [(512, 2), (1536, 2), (2560, 2), (3584, 2)]

    def resid_epi(xsrc, xdst, gate_fn, gb_fn):
        AHEAD = 2

        def pre(order=None):
            return {0: Ring(P, "rx", 4, [128, 512], F32), 1: Ring(P, "ro", 3, [128, 512], F32), "dep": None,
                    "order": order, "idx": 0, "pend": []}
        pre.wants_order = True

        def issue(st, k):
            unit_, tok_ = st["order"][k]
            j_ = unit_[0]
            xo, xob = st[0].next()
            P.dma("sync", xo[:], xsrc[j_ * 128:(j_ + 1) * 128, tok_:tok_ + 512], writes=[xob], sem_buf=xob)
            st["pend"].append((xo, xob))

        def epi(ui, unit, tok, pss, psbs, st):
            j = unit[0]
            g = grp_of_blk(tok // 512)
            n_ = len(st["order"])
            if st["idx"] == 0:
                for k in range(min(AHEAD, n_)):
                    issue(st, k)
            assert st["order"][st["idx"]] == (unit, tok)
            xo, xob = st["pend"].pop(0)
            if st["idx"] + AHEAD < n_:
                issue(st, st["idx"] + AHEAD)
            st["idx"] += 1
            oo, oob = st[1].next()
            if gb_fn is not None:
                P.op("act", lambda e, xo=xo, j=j, g=g: e.activation(out=xo[:], in_=xo[:], func=AF.Identity, bias=gb_fn(j, g), scale=1.0),
                     reads=[xob, CONST], writes=[xob])
            P.op("dve", lambda e, xo=xo, oo=oo, p=pss[0], j=j, g=g: e.scalar_tensor_tensor(
                out=oo[:], in0=p[:], scalar=gate_fn(j, g), in1=xo[:], op0=ALU.mult, op1=ALU.add),
                reads=[psbs[0], xob, CONST], writes=[oob])
            P.dma("sync", xdst[j * 128:(j + 1) * 128, tok:tok + 512], oo[:], reads=[oob], marks=([st["dep"]] if st["dep"] is not None else []), sem_buf=oob)
        return pre, epi

    def phase_ffn(l, xs, xd, last=False, consumer=None):
        sbs = SB44L if last else SB44
        cons = consumer() if consumer is not None else None
        sb_dep = [Buf() for _ in sbs]
        xin_t = P.tile("fx", [128, KC, 1024], BF16)
        xinb = Buf()
        hid = P.tile("fh", [128, FC, 1024], BF16)
        hidb = Buf()
        gur = Ring(P, "fgw", 3, [128, 2, D], BF16)
        dwr = Ring(P, "fdw", 2, [128, FF], BF16)
        sgr = Ring(P, "fs", 2, [128, 512], F32)
        rxr = Ring(P, "frx", 2, [128, 512], F32)
        ror = Ring(P, "fro", 2, [128, 512], F32)
        gps = [[P.psum("fgp") for _ in range(2)] for _ in range(2)]
        gpb = [[Buf() for _ in range(2)] for _ in range(2)]
        dps = [P.psum("fdp") for _ in range(3)]
        dpb = [Buf() for _ in range(3)]
        gi = 0
        di = 0

        def load_x(i):
            tok0, nblk = sbs[i]
            for kc in range(KC):
                P.dma("sync", xin_t[:, kc, 0:nblk * 512], HB2[kc * 128:(kc + 1) * 128, tok0:tok0 + nblk * 512], writes=[xinb], sem_buf=xinb)

        load_x(0)
        for sbi, (tok0, nblk) in enumerate(sbs):
            for j in range(FC):
                wt, wb = gur.next()
                P.dma("pool", wt[:, 0, :], w_gu[l, j], writes=[wb], sem_buf=wb)
                P.dma("pool", wt[:, 1, :], w_gu[l, FC + j], writes=[wb], sem_buf=wb)
                for tb in range(nblk):
                    s_ = gi % 2
                    gi += 1
                    for ci in range(2):
                        for kc in range(KC):
                            P.op("pe", lambda e, p=gps[s_][ci], wt=wt, ci=ci, kc=kc, tb=tb: e.matmul(
                                p[:], lhsT=wt[:, ci, kc * 128:(kc + 1) * 128], rhs=xin_t[:, kc, tb * 512:(tb + 1) * 512],
                                start=(kc == 0), stop=(kc == KC - 1)),
                                reads=[wb, xinb], writes=[gpb[s_][ci]], signal=(kc == KC - 1))
                    sg, sgb = sgr.next()
                    P.op("act", lambda e, sg=sg, p=gps[s_][0]: e.activation(out=sg[:], in_=p[:], func=AF.Silu), reads=[gpb[s_][0]], writes=[sgb])
                    P.op("dve", lambda e, sg=sg, p=gps[s_][1], j=j, tb=tb: e.tensor_tensor(out=hid[:, j, tb * 512:(tb + 1) * 512], in0=p[:], in1=sg[:], op=ALU.mult),
                         reads=[gpb[s_][1], sgb], writes=[hidb])
                    if cons is not None:
                        cons.pop()
            if sbi + 1 < len(sbs):
                load_x(sbi + 1)
            dgroups = [(j_, tok0 + tb_ * 512) for j_ in range(KC) for tb_ in range(nblk)]
            pend = []

            def issue_x(k):
                j_, tok_ = dgroups[k]
                xo_, xob_ = rxr.next()
                P.dma("sync", xo_[:], xs[j_ * 128:(j_ + 1) * 128, tok_:tok_ + 512], writes=[xob_], sem_buf=xob_)
                pend.append((xo_, xob_))
            issue_x(0)
            dk = 0
            for j in range(KC):
                wt, wb = dwr.next()
                P.dma("pool", wt[:], w_down[l, j], writes=[wb], sem_buf=wb)
                for tb in range(nblk):
                    s_ = di % 3
                    di += 1
                    tok = tok0 + tb * 512
                    g = grp_of_blk(tok // 512)
                    xo, xob = pend.pop(0)
                    dk += 1
                    if dk < len(dgroups):
                        issue_x(dk)
                    oo, oob = ror.next()
                    for kc in range(FC):
                        P.op("pe", lambda e, p=dps[s_], wt=wt, kc=kc, tb=tb: e.matmul(
                            p[:], lhsT=wt[:, kc * 128:(kc + 1) * 128], rhs=hid[:, kc, tb * 512:(tb + 1) * 512],
                            start=(kc == 0), stop=(kc == FC - 1)),
                            reads=[wb, hidb], writes=[dpb[s_]], signal=(kc == FC - 1))
                    P.op("dve", lambda e, xo=xo, oo=oo, p=dps[s_], j=j, g=g: e.scalar_tensor_tensor(
                        out=oo[:], in0=p[:], scalar=MODv(l, 5, j, g), in1=xo[:], op0=ALU.mult, op1=ALU.add),
                        reads=[dpb[s_], xob, CONST], writes=[oob])
                    P.dma("sync", xd[j * 128:(j + 1) * 128, tok:tok + 512], oo[:], reads=[oob], marks=[sb_dep[sbi]], sem_buf=oob)
                    if cons is not None:
                        cons.pop()
            if cons is not None:
                cons.add(tok0, nblk * 512, sb_dep[sbi])
        if cons is not None:
            cons.flush()
        P.end_phase()

    def phase_conv(l, xs, xd, last=False, consumer=None):
        jl = l // 2

        def pre():
            return (Ring(P, "cs", 2, [128, 512], F32), Ring(P, "co", 3, [128, 512], F32))

        def epi(ui, unit, tok, pss, psbs, st):
            j = unit[0]
            sg, sgb = st[0].next()
            oo, oob = st[1].next()
            P.op("act", lambda e, sg=sg, p=pss[1], j=j: e.activation(out=sg[:], in_=p[:], func=AF.Sigmoid, bias=cb1[:, jl, 16 + j:17 + j], scale=1.0),
                 reads=[psbs[1], CONST], writes=[sgb])
            P.op("dve", lambda e, sg=sg, oo=oo, p=pss[0], j=j: e.scalar_tensor_tensor(
                out=oo[:], in0=p[:], scalar=cb1[:, jl, j:j + 1], in1=sg[:], op0=ALU.add, op1=ALU.mult),
                reads=[psbs[0], sgb, CONST], writes=[oob])
            P.dma("sync", UG[j * 128:(j + 1) * 128, tok:tok + 512], oo[:], reads=[oob], sem_buf=oob)
        gemm(HB, KC, SB16, [(j, 16 + j) for j in range(KC)], lambda ch: c_pw1[jl, ch], epi, 2, pre=pre, npb=3)

        ur = Ring(P, "cu", 2, [128, TT], F32)
        vr = Ring(P, "cv", 2, [128, TT], F32)
        pcs = [P.tile(f"cpc{i}", [128, 2, 286], BF16) for i in range(2)]
        phs = [P.tile(f"cph{i}", [128, 2, 32, 94], BF16) for i in range(2)]
        pvs = [P.tile(f"cpv{i}", [128, 2, 62, 64], BF16) for i in range(2)]
        pbs = [Buf(), Buf()]
        dgr = Ring(P, "cdg", 2, [128, 31, 128], BF16)
        identb = P.tile("cidb", [128, 128], BF16)
        idb = Buf()
        cwd = P.tile("ccwd", [128, 2, KC, 31], F32)
        P.dma("sync", cwd[:], c_wdw, writes=[idb], sem_buf=idb)
        cps = [P.psum("cps") for _ in range(3)]
        cpb = [Buf() for _ in range(3)]
        P.op("dve", lambda e: e.tensor_copy(out=identb[:], in_=ident[:]), reads=[CONST], writes=[idb])
        for i in range(2):
            P.op("dve", lambda e, i=i: e.memset(pcs[i][:], 0.0), writes=[pbs[i]])
            P.op("dve", lambda e, i=i: e.memset(phs[i][:], 0.0), writes=[pbs[i]])
            P.op("dve", lambda e, i=i: e.memset(pvs[i][:], 0.0), writes=[pbs[i]])
        cgi = 0
        prepd = {}

        def conv_prep(j):
            ut, ub = ur.next()
            pc, ph, pv, pbuf = pcs[j % 2], phs[j % 2], pvs[j % 2], pbs[j % 2]
            dg, dgb = dgr.next()
            P.dma("sync", ut[:], UG[j * 128:(j + 1) * 128, :], writes=[ub], sem_buf=ub)
            uc = ut[:, 0:512].rearrange("p (b t) -> p b t", b=2)
            ux = ut[:, 512:TT].rearrange("p (b r c) -> p b r c", b=2, r=32)
            horiz = j < 8
            P.op("act", lambda e, uc=uc, pc=pc: e.activation(out=pc[:, :, 15:271], in_=uc, func=AF.Identity), reads=[ub], writes=[pbuf])
            for b in range(2):
                if horiz:
                    P.op("act", lambda e, ux=ux, b=b, ph=ph: e.activation(out=ph[:, b, :, 15:79], in_=ux[:, b], func=AF.Identity), reads=[ub], writes=[pbuf])
                else:
                    P.op("act", lambda e, ux=ux, b=b, pv=pv: e.activation(out=pv[:, b, 15:47, :], in_=ux[:, b], func=AF.Identity), reads=[ub], writes=[pbuf])
            for k in range(31):
                P.op("pool", lambda e, dg=dg, k=k, j=j: e.tensor_scalar(out=dg[:, k, :], in0=identb[:], scalar1=cwd[:, jl, j, k:k + 1], scalar2=0.0,
                                                                    op0=ALU.mult, op1=ALU.add), reads=[idb, CONST], writes=[dgb])
            prepd[j] = (pc, ph, pv, pbuf, dg, dgb, horiz)

        mgen = mod_start() if MOD_OVERLAP else None
        conv_prep(0)
        for j in range(KC):
            if j + 1 < KC:
                conv_prep(j + 1)
            if mgen is not None:
                for _ in range(2):
                    if next(mgen, "end") == "end":
                        mgen = None
                        break
            pc, ph, pv, pbuf, dg, dgb, horiz = prepd.pop(j)
            vt, vb = vr.next()
            for blk in range(NBLK):
                ci_ = cgi % 3
                cgi += 1
                ps_, psb_ = cps[ci_], cpb[ci_]
                for k in range(31):
                    if blk == 0:
                        rhs = pc[:, 0:2, k:k + 256]
                        o_ = ps_[:].rearrange("p (b t) -> p b t", b=2)
                    else:
                        b = (blk - 1) // 4
                        r0 = ((blk - 1) % 4) * 8
                        o_ = ps_[:].rearrange("p (r c) -> p r c", r=8)
                        if horiz:
                            rhs = ph[:, b, r0:r0 + 8, k:k + 64]
                        else:
                            rhs = pv[:, b, r0 + k:r0 + k + 8, :]
                    P.op("pe", lambda e, o_=o_, rhs=rhs, dg=dg, k=k: e.matmul(o_, lhsT=dg[:, k, :], rhs=rhs, start=(k == 0), stop=(k == 30)),
                         reads=[dgb, pbuf], writes=[psb_], signal=(k == 30))
                P.op("act", lambda e, vt=vt, ps_=ps_, blk=blk, j=j: e.activation(out=vt[:, blk * 512:(blk + 1) * 512], in_=ps_[:], func=AF.Identity,
                                                                            bias=cbd[:, jl, j:j + 1], scale=1.0),
                     reads=[psb_, CONST], writes=[vb])
            P.dma("sync", VC[j * 128:(j + 1) * 128, :], vt[:], reads=[vb], sem_buf=vb)
        if mgen is not None:
            for _ in mgen:
                pass
        P.end_phase()

        xts = [P.tile(f"lx{i}", [128, KC, 512], F32) for i in range(2)]
        xbs = [[Buf() for _ in range(KC)] for _ in range(2)]
        vb16 = P.tile("lvb", [128, KC, 512], BF16)
        sq = P.tile("lsq", [128, KC, 512], BF16)
        vbb = Buf()
        sqb = Buf()
        hr = Ring(P, "lh", 2, [128, KC, 512], BF16)
        ps1 = P.psum("lps1")
        ps2 = P.psum("lps2")
        pb1 = Buf()
        pb2 = Buf()
        mean = P.tile("lmean", [128, 512], F32)
        rs = P.tile("lrs", [128, 512], F32)
        tmp = P.tile("ltmp", [128, 512], F32)
        rb = Buf()
        srcv = VC.rearrange("(k p) t -> p k t", p=128)
        dstv = HB.rearrange("(k p) t -> p k t", p=128)
        P.dma("sync", xts[0][:], srcv[:, :, 0:512], writes=xbs[0], sem_buf=xbs[0][0])
        mgen2 = mod_start() if MOD_OVERLAP else None
        for blk in range(NBLK):
            t0 = blk * 512
            xt, xb = xts[blk % 2], xbs[blk % 2]
            ht, hb = hr.next()
            if blk + 1 < NBLK:
                P.dma("sync", xts[(blk + 1) % 2][:], srcv[:, :, t0 + 512:t0 + 1024], writes=xbs[(blk + 1) % 2], sem_buf=xbs[(blk + 1) % 2][0])
            if mgen2 is not None:
                for _ in range(3):
                    if next(mgen2, "end") == "end":
                        mgen2 = None
                        break
            P.op("act", lambda e, xt=xt: e.activation(out=vb16[:], in_=xt[:], func=AF.Identity), reads=xb, writes=[vbb])
            P.op("pool", lambda e, xt=xt: e.tensor_tensor(out=sq[:], in0=xt[:], in1=xt[:], op=ALU.mult), reads=xb, writes=[sqb])
            for kc in range(KC):
                P.op("pe", lambda e, kc=kc: e.matmul(ps1[:], lhsT=ones_b[:], rhs=vb16[:, kc, :], start=(kc == 0), stop=(kc == KC - 1)),
                     reads=[vbb, CONST], writes=[pb1], signal=(kc == KC - 1))
            P.op("act", lambda e: e.activation(out=mean[:], in_=ps1[:], func=AF.Identity, scale=1.0 / D), reads=[pb1], writes=[rb])
            P.op("dve", lambda e: e.tensor_tensor(out=tmp[:], in0=mean[:], in1=mean[:], op=ALU.mult), reads=[rb], writes=[rb])
            for kc in range(KC):
                P.op("pe", lambda e, kc=kc: e.matmul(ps2[:], lhsT=ones_b[:], rhs=sq[:, kc, :], start=(kc == 0), stop=(kc == KC - 1)),
                     reads=[sqb, CONST], writes=[pb2], signal=(kc == KC - 1))
            P.op("dve", lambda e: e.scalar_tensor_tensor(out=rs[:], in0=ps2[:], scalar=1.0 / D, in1=tmp[:], op0=ALU.mult, op1=ALU.subtract),
                 reads=[pb2, rb], writes=[rb])
            P.op("act", lambda e: e.activation(out=rs[:], in_=rs[:], func=AF.Sqrt, bias=epst[:, 0:1], scale=1.0), reads=[rb, CONST], writes=[rb])
            P.op("dve", lambda e: e.reciprocal(out=rs[:], in_=rs[:]), reads=[rb], writes=[rb])
            for kc in range(KC):
                P.op("dve", lambda e, xt=xt, kc=kc: e.tensor_tensor(out=xt[:, kc, :], in0=xt[:, kc, :], in1=mean[:], op=ALU.subtract),
                     reads=[xb[kc], rb], writes=[xb[kc]])
                P.op("pool", lambda e, xt=xt, kc=kc: e.tensor_tensor(out=xt[:, kc, :], in0=xt[:, kc, :], in1=rs[:], op=ALU.mult),
                     reads=[xb[kc], rb], writes=[xb[kc]])
                P.op("act", lambda e, xt=xt, ht=ht, kc=kc: e.activation(out=ht[:, kc, :], in_=xt[:, kc, :], func=AF.Silu,
                                                                       bias=clb[:, jl, kc:kc + 1], scale=clg[:, jl, kc:kc + 1]),
                     reads=[xb[kc], CONST], writes=[hb])
            P.dma("sync", dstv[:, :, t0:t0 + 512], ht[:], reads=[hb], sem_buf=hb)
        if mgen2 is not None:
            for _ in mgen2:
                pass
        P.end_phase()

        pre2, epi2 = resid_epi(xs, xd, lambda j, g: MODv(l, 2, j, g), lambda j, g: GB1[:, l, j, g:g + 1])
        gemm(HB, KC, SB16L if last else SB16, [(j,) for j in range(KC)], lambda ch: c_pw2[jl, ch], epi2, 1, pre=pre2, consumer=consumer, npb=4)

    def phase_mlstm(l, xs, xd, last=False, consumer=None):
        jl = l // 2

        def pre():
            wg = P.tile("mwg", [128, KC, 16], BF16)
            wgb = Buf()
            P.dma("pool", wg[:], m_wg[:, jl], writes=[wgb], sem_buf=wgb)
            return (Ring(P, "mo", 3, [128, 512], BF16), wg, wgb, Ring(P, "mg", 2, [4, 512], F32))

        def epi(ui, unit, tok, pss, psbs, st):
            ch = unit[0]
            oo, oob = st[0].next()
            if ch < 8:
                dstap = QD[ch * 128:(ch + 1) * 128, tok:tok + 512]
                P.op("act", lambda e, oo=oo, p=pss[0]: e.activation(out=oo[:], in_=p[:], func=AF.Identity), reads=[psbs[0]], writes=[oob])
            elif ch < 16:
                dstap = KD[(ch - 8) * 128:(ch - 7) * 128, tok:tok + 512]
                P.op("act", lambda e, oo=oo, p=pss[0]: e.activation(out=oo[:], in_=p[:], func=AF.Identity, scale=1.0 / 16.0), reads=[psbs[0]], writes=[oob])
            else:
                dstap = SO[(ch - 32) * 128:(ch - 31) * 128, tok:tok + 512]
                P.op("act", lambda e, oo=oo, p=pss[0]: e.activation(out=oo[:], in_=p[:], func=AF.Sigmoid), reads=[psbs[0]], writes=[oob])
            P.dma("sync", dstap, oo[:], reads=[oob], sem_buf=oob)

        def extra(tok0, nblk, intile, inb, wr, ps, psb, st):
            oring, wg, wgb, gring = st
            gi = 0
            for vg in range(4):
                wt, wb = wr.next()
                P.dma("pool", wt[:], m_win[jl, 16 + vg * 4:20 + vg * 4].rearrange("n p k -> p n k"), writes=[wb], sem_buf=wb)
                for t128 in range(nblk * 4):
                    s = gi % len(ps)
                    gi += 1
                    p_, pb_ = ps[s][0], psb[s][0]
                    for kc in range(KC):
                        P.op("pe", lambda e, p_=p_, wt=wt, kc=kc, t128=t128: e.matmul(
                            p_[:].rearrange("p (a c) -> p a c", a=4), lhsT=intile[:, kc, t128 * 128:(t128 + 1) * 128], rhs=wt[:, 0:4, kc * 128:(kc + 1) * 128],
                            start=(kc == 0), stop=(kc == KC - 1)), reads=[wb, inb], writes=[pb_], signal=(kc == KC - 1))
                    oo, oob = oring.next()
                    P.op("act", lambda e, oo=oo, p_=p_: e.activation(out=oo[:], in_=p_[:], func=AF.Identity), reads=[pb_], writes=[oob])
                    r0 = tok0 + t128 * 128
                    P.dma("sync", VD[r0:r0 + 128, vg * 512:(vg + 1) * 512], oo[:], reads=[oob], sem_buf=oob)
            for tb in range(nblk):
                for ty in range(4):
                    s = gi % len(ps)
                    gi += 1
                    p_, pb_ = ps[s][0], psb[s][0]
                    for kc in range(KC):
                        P.op("pe", lambda e, p_=p_, kc=kc, tb=tb, ty=ty: e.matmul(
                            p_[0:4, :], lhsT=wg[:, kc, ty * 4:(ty + 1) * 4], rhs=intile[:, kc, tb * 512:(tb + 1) * 512],
                            start=(kc == 0), stop=(kc == KC - 1)), reads=[wgb, inb], writes=[pb_], signal=(kc == KC - 1))
                    go, gob = gring.next()
                    P.op("act", lambda e, go=go, p_=p_: e.activation(out=go[:], in_=p_[0:4, :], func=AF.Identity), reads=[pb_], writes=[gob])
                    P.dma("sync", GD[ty, :, tok0 + tb * 512:tok0 + (tb + 1) * 512], go[:], reads=[gob], sem_buf=gob)

        units = [(ch,) for ch in list(range(16)) + list(range(32, 48))]
        gemm(HB, KC, SB16, units, lambda ch: m_win[jl, ch], epi, 4, extra=extra, pre=pre, npb=4)

        masks = P.tile("amask", [128, 8, 512], F32)
        mkb = Buf()
        P.dma("sync", masks[:], masksd, writes=[mkb], sem_buf=mkb)
        sel = P.tile("asel", [4, 4, 128], F32)
        P.dma("sync", sel[:], seld, writes=[mkb], sem_buf=mkb)
        qt = P.tile("aq", [128, 2, 2304], BF16)
        kt = P.tile("ak", [128, 2, 2304], BF16)
        vt = P.tile("av", [128, 18, 512], BF16)
        qkvb = Buf()
        brow2 = [P.tile(f"abrow{i}", [128, 2304], F32) for i in range(2)]
        brb2 = [Buf(), Buf()]
        ut = P.tile("aut", [128, 18, 8], F32)
        utb = Buf()
        G2 = P.tile("ag2", [4, 3, 2304], F32)
        gtb = Buf()
        bsum = [P.tile(f"ab{i}", [4, 2304], F32) for i in range(2)]
        bsb = Buf()
        bgs = P.tile("abgs", [4, 4], F32)
        er = Ring(P, "ae", 3, [128, 512], F32)
        tr = Ring(P, "at", 2, [128, 512], F32)
        pr = Ring(P, "ap", 3, [128, 512], BF16)
        rdr = Ring(P, "arden", 3, [128, 512], F32)
        nr = Ring(P, "ansb", 4, [128, 4, 512], F32)
        hsq = P.tile("ahsq", [128, 4, 512], BF16)
        hqb = Buf()
        hrs = P.tile("ahrs", [128, 512], F32)
        hrb = Buf()
        sor = Ring(P, "aso", 2, [128, 4, 512], BF16)
        hor = Ring(P, "aho", 2, [128, 4, 512], BF16)
        stp = [P.psum("ast") for _ in range(2)]
        stb = [Buf() for _ in range(2)]
        nump = [P.psum("anum") for _ in range(4)]
        denp = P.psum("aden")
        accb = Buf()
        misc = P.psum("amisc")
        mib = Buf()
        sti = 0
        tail_q = []

        def acquire(ring):
            t_, b_ = ring.next()
            lastref = -1
            for qi, (_, bufs) in enumerate(tail_q):
                if any(x is b_ for x in bufs):
                    lastref = qi
            for _ in range(lastref + 1):
                tail_q.pop(0)[0]()
            return t_, b_
        P.op("dve", lambda e: e.memset(G2[:, 2, :], 1.0), writes=[gtb])

        for b in range(2):
            cx = (b * 256, 256)
            xx = (512 + b * 2048, 2048)
            for d_ in range(2):
                segs = [(0, cx), (256, xx)] if d_ == 0 else [(0, xx), (2048, cx)]
                for sl, ty in ((0, 2 * d_), (1, 2 * d_ + 1)):
                    for (o, (s0, n)) in segs:
                        P.dma("sync", G2[:, sl, o:o + n], GD[ty, :, s0:s0 + n], writes=[gtb], sem_buf=gtb)
                    P.op("dve", lambda e, sl=sl, ty=ty: e.tensor_scalar(out=G2[:, sl, :], in0=G2[:, sl, :], scalar1=mbg[:, jl, ty:ty + 1], scalar2=None, op0=ALU.add),
                         reads=[gtb, CONST], writes=[gtb])
                    P.op("act", lambda e, sl=sl: e.activation(out=G2[:, sl, :], in_=G2[:, sl, :], func=AF.Tanh, scale=1.0 / 15.0), reads=[gtb], writes=[gtb])
                P.op("act", lambda e: e.activation(out=G2[:, 1, :], in_=G2[:, 1, :], func=AF.Sigmoid, scale=15.0), reads=[gtb], writes=[gtb])
                P.op("act", lambda e: e.activation(out=G2[:, 1, :], in_=G2[:, 1, :], func=AF.Ln), reads=[gtb], writes=[gtb])
                P.op("act", lambda e: e.activation(out=G2[:, 0, :], in_=G2[:, 0, :], func=AF.Identity, scale=15.0), reads=[gtb], writes=[gtb])
                P.op("dve", lambda e, d_=d_: e.tensor_tensor_scan(out=bsum[d_][:], data0=G2[:, 2, :], data1=G2[:, 1, :], initial=0.0, op0=ALU.mult, op1=ALU.add),
                     reads=[gtb], writes=[bsb])
                if d_ == 1:
                    P.op("dve", lambda e: e.tensor_copy(out=bgs[:, 0:1], in_=bsum[1][:, 2303:2304]), reads=[bsb], writes=[bsb])
                    P.op("dve", lambda e: e.tensor_tensor(out=bsum[1][:], in0=G2[:, 1, :], in1=bsum[1][:], op=ALU.subtract), reads=[gtb, bsb], writes=[bsb])
                    P.op("dve", lambda e: e.tensor_scalar(out=bsum[1][:], in0=bsum[1][:], scalar1=bgs[:, 0:1], scalar2=None, op0=ALU.add), reads=[bsb], writes=[bsb])
                P.op("dve", lambda e, d_=d_: e.tensor_tensor(out=G2[:, 0, :], in0=G2[:, 0, :], in1=bsum[d_][:], op=ALU.subtract), reads=[gtb, bsb], writes=[gtb])
                for fb in range(18):
                    lb = fb if d_ == 0 else (fb - 2) % 18
                    P.op("pe", lambda e, fb=fb, lb=lb: e.matmul(misc[:, fb * 4:fb * 4 + 4], lhsT=G2[:, 0, lb * 128:(lb + 1) * 128],
                                                          rhs=ident[0:4, 0:4], start=True, stop=True),
                         reads=[gtb, CONST], writes=[mib], signal=(fb == 17))
                P.op("act", lambda e, d_=d_: e.activation(out=ut[:, :, d_ * 4:d_ * 4 + 4], in_=misc[:, 0:72].rearrange("p (f h) -> p f h", h=4), func=AF.Identity),
                     reads=[mib], writes=[utb, mib])
            for h in range(4):
                for (o, (s0, n)) in [(0, cx), (256, xx)]:
                    for dc in range(2):
                        r0 = h * 256 + dc * 128
                        P.dma("sync", qt[:, dc, o:o + n], QD[r0:r0 + 128, s0:s0 + n], writes=[qkvb], sem_buf=qkvb)
                        P.dma("sync", kt[:, dc, o:o + n], KD[r0:r0 + 128, s0:s0 + n], writes=[qkvb], sem_buf=qkvb)
                    P.dma("sync", vt[:, o // 128:(o + n) // 128, :], VD[s0:s0 + n, h * 512:(h + 1) * 512].rearrange("(f p) e -> p f e", p=128),
                          writes=[qkvb], sem_buf=qkvb)
                for d_ in range(2):
                    for c0 in range(0, 2304, 512):
                        cn = min(512, 2304 - c0)
                        P.op("pe", lambda e, d_=d_, c0=c0, cn=cn, h=h: e.matmul(misc[:, 0:cn], lhsT=sel[:, h, :], rhs=bsum[d_][:, c0:c0 + cn], start=True, stop=True),
                             reads=[bsb, mkb], writes=[mib])
                        P.op("act", lambda e, d_=d_, c0=c0, cn=cn: e.activation(out=brow2[d_][:, c0:c0 + cn], in_=misc[:, 0:cn], func=AF.Identity),
                             reads=[mib], writes=[brb2[d_], mib])
                pairs = []
                for ti in range(1 if last else 0, 5):
                    ffb0, ffb1 = (0, 2) if ti == 0 else (2 + 4 * (ti - 1), 6 + 4 * (ti - 1))
                    for d_ in range(2):
                        if d_ == 0:
                            lb0, lb1 = ffb0, ffb1
                            sbl = list(range(0, lb1))
                        else:
                            lb0, lb1 = ((16, 18) if ti == 0 else (4 * (ti - 1), 4 * ti))
                            sbl = list(range(17, lb0 - 1, -1))
                        for si, sb_ in enumerate(sbl):
                            pairs.append(dict(d=d_, ti=ti, lb0=lb0, lb1=lb1, sb=sb_, first=(si == 0), last=(si == len(sbl) - 1), fc0=ffb0 * 128))

                def stageA(p):
                    nonlocal sti
                    d_ = p["d"]
                    lb0, lb1, sb_ = p["lb0"], p["lb1"], p["sb"]
                    tc = (lb1 - lb0) * 128
                    lc0 = lb0 * 128
                    fc0 = p["fc0"]
                    fsb = sb_ if d_ == 0 else (sb_ + 2) % 18
                    diag = lb0 <= sb_ < lb1
                    s_ = sti % 2
                    sti += 1
                    p.update(tc=tc, fsb=fsb, s_=s_)
                    for dc in range(2):
                        P.op("pe", lambda e, s_=s_, dc=dc, fsb=fsb, fc0=fc0, tc=tc: e.matmul(
                            stp[s_][:, 0:tc], lhsT=kt[:, dc, fsb * 128:(fsb + 1) * 128], rhs=qt[:, dc, fc0:fc0 + tc],
                            start=(dc == 0), stop=(dc == 1)), reads=[qkvb], writes=[stb[s_]], signal=(dc == 1))
                    et, eb = er.next()
                    p.update(et=et, eb=eb)
                    ucol = ut[:, fsb, d_ * 4 + h:d_ * 4 + h + 1]
                    br = brow2[d_]
                    if diag:
                        tt_, ttb = tr.next()
                        mk = masks[:, d_ * 4 + (sb_ - lb0), 0:tc]
                        P.op("dve", lambda e, tt_=tt_, ucol=ucol, mk=mk, lc0=lc0, tc=tc, br=br: e.scalar_tensor_tensor(
                            out=tt_[:, 0:tc], in0=br[:, lc0:lc0 + tc], scalar=ucol, in1=mk, op0=ALU.add, op1=ALU.add),
                            reads=[brb2[d_], utb, mkb], writes=[ttb])
                        P.op("act", lambda e, et=et, tt_=tt_, tc=tc: e.activation(out=et[:, 0:tc], in_=tt_[:, 0:tc], func=AF.Exp),
                             reads=[ttb], writes=[eb])
                    else:
                        P.op("act", lambda e, et=et, ucol=ucol, lc0=lc0, tc=tc, br=br: e.activation(
                            out=et[:, 0:tc], in_=br[:, lc0:lc0 + tc], func=AF.Exp, bias=ucol, scale=1.0),
                            reads=[brb2[d_], utb], writes=[eb])

                hf_state = {}

                def stageBC(p):
                    d_, tc, fc0, fsb, s_, et, eb = p["d"], p["tc"], p["fc0"], p["fsb"], p["s_"], p["et"], p["eb"]
                    pt, ptb = pr.next()
                    P.op("dve", lambda e, pt=pt, et=et, s_=s_, tc=tc: e.tensor_tensor(out=pt[:, 0:tc], in0=stp[s_][:, 0:tc], in1=et[:, 0:tc], op=ALU.mult),
                         reads=[stb[s_], eb], writes=[ptb])
                    first, last = p["first"], p["last"]
                    for ec in range(4):
                        P.op("pe", lambda e, ec=ec, pt=pt, fsb=fsb, tc=tc, first=first, last=last: e.matmul(
                            nump[ec][:, 0:tc], lhsT=vt[:, fsb, ec * 128:(ec + 1) * 128], rhs=pt[:, 0:tc], start=first, stop=last),
                            reads=[qkvb, ptb], writes=[accb], signal=False)
                    P.op("pe", lambda e, pt=pt, tc=tc, first=first, last=last: e.matmul(
                        denp[:, 0:tc], lhsT=ones_b[:], rhs=pt[:, 0:tc], start=first, stop=last),
                        reads=[CONST, ptb], writes=[accb], signal=True)
                    if not last:
                        return
                    nsb, nsbb = acquire(nr)
                    rden, rdb = acquire(rdr)
                    P.op("act", lambda e, tc=tc, rden=rden: e.activation(out=rden[:, 0:tc], in_=denp[:, 0:tc], func=AF.Abs),
                         reads=[accb], writes=[rdb])
                    for ec in range(4):
                        P.op("act", lambda e, ec=ec, nsb=nsb, tc=tc: e.activation(out=nsb[:, ec, 0:tc], in_=nump[ec][:, 0:tc], func=AF.Identity),
                             reads=[accb], writes=[nsbb])

                    def s_rden():
                        P.op("dve", lambda e: e.tensor_scalar_max(out=rden[:, 0:tc], in0=rden[:, 0:tc], scalar1=1.0), reads=[rdb], writes=[rdb])
                        P.op("dve", lambda e: e.reciprocal(out=rden[:, 0:tc], in_=rden[:, 0:tc]), reads=[rdb], writes=[rdb])

                    def s_scale():
                        for ec in range(4):
                            P.op("pool", lambda e, ec=ec: e.tensor_tensor(out=nsb[:, ec, 0:tc], in0=nsb[:, ec, 0:tc], in1=rden[:, 0:tc], op=ALU.mult),
                                 reads=[nsbb, rdb], writes=[nsbb])
                    nop_ = (lambda: None, ())
                    tail_q.extend([(s_rden, (rdb,)), nop_, (s_scale, (rdb, nsbb)), nop_, nop_])
                    if d_ == 0:
                        hf_state["f"] = (nsb, nsbb)
                        return
                    hf, hfb = hf_state.pop("f")
                    ctok = (b * 256) if p["ti"] == 0 else (512 + b * 2048 + (p["ti"] - 1) * 512)
                    hh = h
                    st_ = {}

                    def s_add():
                        for ec in range(4):
                            P.op("pool", lambda e, ec=ec: e.tensor_tensor(out=hf[:, ec, 0:tc], in0=hf[:, ec, 0:tc], in1=nsb[:, ec, 0:tc], op=ALU.add),
                                 reads=[nsbb, hfb], writes=[hfb])

                    def s_sq():
                        st_["so"] = acquire(sor)
                        sot, sob = st_["so"]
                        P.dma("sync", sot[:, :, 0:tc], SO[hh * 512:(hh + 1) * 512, ctok:ctok + tc].rearrange("(e p) t -> p e t", p=128), writes=[sob], sem_buf=sob)
                        P.op("act", lambda e: e.activation(out=hsq[:, :, 0:tc], in_=hf[:, :, 0:tc], func=AF.Square), reads=[hfb], writes=[hqb])

                    def s_ssq():
                        for ec in range(4):
                            P.op("pe", lambda e, ec=ec: e.matmul(misc[:, 0:tc], lhsT=ones_b[:], rhs=hsq[:, ec, 0:tc], start=(ec == 0), stop=(ec == 3)),
                                 reads=[hqb, CONST], writes=[mib], signal=(ec == 3))

                    def s_sqrt():
                        P.op("act", lambda e: e.activation(out=hrs[:, 0:tc], in_=misc[:, 0:tc], func=AF.Sqrt, bias=epst[:, 0:1], scale=1.0 / 512.0),
                             reads=[mib, CONST], writes=[hrb, mib])

                    def s_rstd():
                        P.op("dve", lambda e: e.reciprocal(out=hrs[:, 0:tc], in_=hrs[:, 0:tc]), reads=[hrb], writes=[hrb])

                    def s_norm():
                        for ec in range(4):
                            fch = hh * 4 + ec
                            P.op("dve", lambda e, ec=ec, fch=fch: e.scalar_tensor_tensor(out=hf[:, ec, 0:tc], in0=hf[:, ec, 0:tc], scalar=mhg[:, jl, fch:fch + 1], in1=hrs[:, 0:tc], op0=ALU.mult, op1=ALU.mult),
                                 reads=[hfb, hrb, CONST], writes=[hfb])

                    def s_gate():
                        sot, sob = st_["so"]
                        hot, hob = acquire(hor)
                        for ec in range(4):
                            P.op("pool", lambda e, ec=ec: e.tensor_tensor(out=hot[:, ec, 0:tc], in0=hf[:, ec, 0:tc], in1=sot[:, ec, 0:tc], op=ALU.mult),
                                 reads=[hfb, sob], writes=[hob])
                        P.dma("sync", HB[hh * 512:(hh + 1) * 512, ctok:ctok + tc].rearrange("(e p) t -> p e t", p=128), hot[:, :, 0:tc], reads=[hob], sem_buf=hob)
                    tail_q.extend([(s_add, (nsbb, hfb)), nop_, nop_, (s_sq, (hfb,)), (s_ssq, ()), (s_sqrt, ()), nop_, (s_rstd, ()), nop_, (s_norm, (hfb,)), (s_gate, (hfb,))])

                stageA(pairs[0])
                for i in range(len(pairs)):
                    if i + 1 < len(pairs):
                        stageA(pairs[i + 1])
                    stageBC(pairs[i])
                    if tail_q:
                        tail_q.pop(0)[0]()
        while tail_q:
            tail_q.pop(0)[0]()
        P.end_phase()

        pre2, epi2 = resid_epi(xs, xd, lambda j, g: MODv(l, 2, j, g), None)
        gemm(HB, KC, SB16L if last else SB16, [(j,) for j in range(KC)], lambda ch: m_wout[jl, ch], epi2, 1, pre=pre2, consumer=consumer, npb=4)

    epst = P.gtile("epst", [128, 1], F32)
    P.op("dve", lambda e: e.memset(epst[:], EPS), writes=[CONST])
    phase_mod()
    xcur = xin
    nxt = 0
    lys = list(layers if layers is not None else range(depth))
    for li, l in enumerate(lys):
        if li > 0:
            P.new_epoch()
        lastl = (li == len(lys) - 1)
        if li == 0:
            phase_norm(xcur, lambda kc, g, l=l: A1[:, l, kc, g:g + 1], lambda kc, g, l=l: MODv(l, 0, kc, g), HB, BF16)
        xmid = XR[nxt]
        c2 = (lambda l=l, xmid=xmid: NormConsumer(xmid, lambda kc, g: A2[:, l, kc, g:g + 1], lambda kc, g: MODv(l, 3, kc, g), HB2, BF16, 0, 512))
        if l % 2 == 0:
            phase_conv(l, xcur, xmid, last=lastl, consumer=c2)
        else:
            phase_mlstm(l, xcur, xmid, last=lastl, consumer=c2)
        nxt = 1 - nxt
        xnew = XR[nxt]
        if lastl:
            c1 = (lambda xnew=xnew: NormConsumer(xnew, lambda kc, g: fint[:, kc:kc + 1], lambda kc, g: None, out, F32, 512, 128))
        else:
            ln = lys[li + 1]
            c1 = (lambda xnew=xnew, ln=ln: NormConsumer(xnew, lambda kc, g: A1[:, ln, kc, g:g + 1], lambda kc, g: MODv(ln, 0, kc, g), HB, BF16, 0, 128))
        phase_ffn(l, xmid, xnew, last=lastl, consumer=c1)
        xcur = xnew
        nxt = 1 - nxt
    P.finish()
    return nc


def _chunk_layout(w):
    K, N = w.shape
    return np.ascontiguousarray(w.reshape(K // 128, 128, N // 128, 128).transpose(2, 1, 0, 3).reshape(N // 128, 128, K))


def _vec_layout(v):
    L, N = v.shape
    return np.ascontiguousarray(v.reshape(L, N // 128, 128).transpose(2, 0, 1))


def _consts():
    ident = np.eye(128, dtype=np.float32)
    masks = np.full((128, 8, 512), NEG, dtype=np.float32)
    s = np.arange(128)[:, None]
    c = np.arange(512)[None, :]
    for j in range(4):
        masks[:, j, :] = np.where(c >= j * 128 + s, 0.0, NEG)
        masks[:, 4 + j, :] = np.where(c <= j * 128 + s, 0.0, NEG)
    sel = np.zeros((4, 4, 128), dtype=np.float32)
    for h in range(4):
        sel[h, h, :] = 1.0
    return ident, masks, sel


def prep_shared(inp, depth=4):
    f = lambda a: np.asarray(a, dtype=np.float32)
    sh = {}
    sh["w_mod"] = np.stack([_chunk_layout(f(inp["w_mod"][l])) for l in range(4)])
    sh["b_mod"] = _vec_layout(f(inp["b_mod"]))
    sh["w_gu"] = np.stack([_chunk_layout(f(inp["w_gu"][l])) for l in range(4)])
    sh["w_down"] = np.stack([_chunk_layout(f(inp["w_down"][l])) for l in range(4)])
    sh["n1g"] = _vec_layout(f(inp["norm1_g"]))
    sh["n2g"] = _vec_layout(f(inp["norm2_g"]))
    sh["fing"] = _vec_layout(f(inp["final_g"])[None])[:, 0, :].copy()
    sh["c_pw1"] = np.stack([_chunk_layout(f(inp["conv_w_pw1"][j])) for j in range(2)])
    sh["c_bpw1"] = _vec_layout(f(inp["conv_b_pw1"]))
    wdw = f(inp["conv_w_dw"])
    sh["c_wdw"] = np.ascontiguousarray(wdw.reshape(2, 31, 16, 128).transpose(3, 0, 2, 1))
    sh["c_bdw"] = _vec_layout(f(inp["conv_b_dw"]))
    sh["c_lng"] = _vec_layout(f(inp["conv_ln_g"]))
    sh["c_lnb"] = _vec_layout(f(inp["conv_ln_b"]))
    sh["c_pw2"] = np.stack([_chunk_layout(f(inp["conv_w_pw2"][j])) for j in range(2)])
    sh["c_bpw2"] = _vec_layout(f(inp["conv_b_pw2"]))
    win = f(inp["m_w_in"])
    sh["m_win"] = np.stack([_chunk_layout(win[j][:, :6144]) for j in range(2)])
    wg = win[:, :, 6144:6160]
    sh["m_wg"] = np.ascontiguousarray(wg.reshape(2, 16, 128, 16).transpose(2, 0, 1, 3))
    bg = f(inp["m_b_gates"])
    sh["m_bg"] = np.ascontiguousarray(bg.reshape(2, 4, 4).transpose(2, 0, 1))
    sh["m_hng"] = _vec_layout(f(inp["m_hn_g"]))
    sh["m_wout"] = np.stack([_chunk_layout(f(inp["m_w_out"][j])) for j in range(2)])
    ident, masks, sel = _consts()
    sh["ident"] = ident
    sh["masks"] = masks
    sh["sel"] = sel
    return sh


def prep_core(inp, c):
    f = lambda a: np.asarray(a, dtype=np.float32)
    b0, b1 = 2 * c, 2 * c + 1
    x = inp["x"]
    ctx = inp["ctx"]
    xin = np.concatenate([f(ctx[b0]).T, f(ctx[b1]).T, f(x[b0]).T, f(x[b1]).T], axis=1)
    cv = np.stack([f(inp["c"][b0]), f(inp["c"][b1]), f(inp["c_ctx"])], axis=0)
    cT = np.ascontiguousarray(cv.reshape(3, 16, 128).transpose(2, 1, 0))
    return {"xin": np.ascontiguousarray(xin), "cT": cT}


_NC_CACHE = {}


def kernel(**inputs):
    n = 8
    if "nc" not in _NC_CACHE:
        _NC_CACHE["nc"] = build(4)
    nc = _NC_CACHE["nc"]
    sh = prep_shared(inputs)
    in_maps = []
    for c in range(n):
        m = dict(sh)
        m.update(prep_core(inputs, c))
        in_maps.append(m)
    res = run_bass_kernel_spmd(nc, in_maps, core_ids=list(range(n)))
    outs = []
    for c in range(n):
        o = np.asarray(res.results[c]["out"])
        outs.append(o[:, :2048].T)
        outs.append(o[:, 2048:].T)
    return np.ascontiguousarray(np.stack(outs, axis=0).astype(np.float32))
```

```python
import contextlib
import numpy as np
import concourse.bass as bass
import concourse.mybir as mybir
from concourse.bass_utils import run_bass_kernel_spmd

F32 = mybir.dt.float32
BF16 = mybir.dt.bfloat16
AF = mybir.ActivationFunctionType
ALU = mybir.AluOpType

D = 2048
FF = 5632
TT = 4608
NBLK = 9
KC = 16
FC = 44
EPS = 1e-6
NEG = -30000.0
STRICT = True
MOD_OVERLAP = True


def grp_of_blk(blk):
    return 2 if blk == 0 else (0 if blk <= 4 else 1)


class Sem:
    def __init__(self, h):
        self.h = h
        self.n = 0


class Buf:
    __slots__ = ("ready", "free", "dsem")

    def __init__(self):
        self.ready = {}
        self.free = {}
        self.dsem = None


class Prog:
    ENG = ("sync", "act", "dve", "pool", "pe")

    def __init__(self, nc, n_dsem=48, strict=True):
        self.nc = nc
        self.strict = strict
        self.gstack = contextlib.ExitStack()
        self.nsem = 0
        self.nt = 0
        self.esem = {}
        self.dpool = [self._sem("d") for _ in range(n_dsem)]
        self.dspool = [self._sem("ds") for _ in range(6)]
        self.bar = self._sem("bar")
        self.bar_t = self.gstack.enter_context(nc.sbuf_tensor("bar_t", [128, 8], F32))
        self.new_epoch()
        self._begin()
        self.first = True

    def _sem(self, name):
        h = self.gstack.enter_context(self.nc.semaphore(f"s{self.nsem}_{name}"))
        self.nsem += 1
        return Sem(h)

    def new_epoch(self):
        for e in ("act", "dve", "pool", "pe"):
            self.esem[e] = self._sem(e)

    def _begin(self):
        self.q = {e: [] for e in self.ENG}
        self.pstack = contextlib.ExitStack()
        self.di = 0
        self.dsi = 0
        self.used_d = []

    def tile(self, name, shape, dt):
        self.nt += 1
        return self.pstack.enter_context(self.nc.sbuf_tensor(f"{name}_{self.nt}", list(shape), dt))

    def gtile(self, name, shape, dt):
        self.nt += 1
        return self.gstack.enter_context(self.nc.sbuf_tensor(f"{name}_{self.nt}", list(shape), dt))

    def psum(self, name, shape=(128, 512), dt=F32):
        self.nt += 1
        return self.pstack.enter_context(self.nc.psum_tensor(f"{name}_{self.nt}", list(shape), dt))

    def _dsem(self, buf, eng):
        if buf.dsem is None:
            if eng == "pool":
                assert self.dsi < len(self.dspool), "out of sw dma semaphores"
                buf.dsem = self.dspool[self.dsi]
                self.dsi += 1
            else:
                assert self.di < len(self.dpool), "out of dma semaphores"
                buf.dsem = self.dpool[self.di]
                self.di += 1
            self.used_d.append(buf.dsem)
        return buf.dsem

    def op(self, eng, fn, reads=(), writes=(), dma_buf=None, signal=True, marks=()):
        if dma_buf is not None:
            sem = self._dsem(dma_buf, eng)
            amt = 16
            own = None
        else:
            sem = self.esem[eng]
            amt = 1
            own = sem if (eng == "pe" or not self.strict) else None
        waits = {}
        for b in reads:
            for s, v in b.ready.items():
                if s is not own:
                    waits[s] = max(waits.get(s, 0), v)
        for b in writes:
            for s, v in list(b.ready.items()) + list(b.free.items()):
                if s is not own:
                    waits[s] = max(waits.get(s, 0), v)
        if signal:
            sem.n += amt
            val = sem.n
            inc = (sem, amt)
        else:
            val = sem.n + amt
            inc = None
        for b in reads:
            b.free[sem] = max(b.free.get(sem, 0), val)
        for b in writes:
            b.ready[sem] = max(b.ready.get(sem, 0), val)
        for b in marks:
            b.ready[sem] = max(b.ready.get(sem, 0), val)
        self.q[eng].append((fn, list(waits.items()), inc))

    def dma(self, eng, out, in_, reads=(), writes=(), sem_buf=None, marks=()):
        self.op(eng, lambda e: e.dma_start(out=out, in_=in_), reads=reads, writes=writes, dma_buf=sem_buf, marks=marks)

    def end_phase(self):
        waits = [(s, s.n) for s in self.esem.values() if s is not self.esem["dve"] and s.n > 0]
        waits += [(s, s.n) for s in self.used_d if s.n > 0]
        self.bar.n += 1
        bt = self.bar_t
        self.q["dve"].append((lambda e: e.memset(bt[0:1, 0:1], 0.0), waits, (self.bar, 1)))
        first = self.first
        barv = self.bar.n - 1
        qs = self.q
        with self.nc.Block() as block:
            def mk(items):
                def run(e):
                    seen = {}
                    if not first:
                        e.wait_ge(self.bar.h, barv)
                    for fn, waits, inc in items:
                        for s, v in waits:
                            if v <= 0 or seen.get(id(s), 0) >= v:
                                continue
                            e.wait_ge(s.h, v)
                            seen[id(s)] = v
                        ins = fn(e)
                        if inc is not None:
                            ins.then_inc(inc[0].h, inc[1])
                return run
            block.sync(mk(qs["sync"]))
            block.scalar(mk(qs["act"]))
            block.vector(mk(qs["dve"]))
            block.gpsimd(mk(qs["pool"]))
            block.tensor(mk(qs["pe"]))
        self.pstack.close()
        self.first = False
        self._begin()

    def finish(self):
        barv = self.bar.n
        with self.nc.Block() as block:
            block.sync(lambda e: e.wait_ge(self.bar.h, barv))
        self.gstack.close()


class Ring:
    def __init__(self, P, name, n, shape, dt):
        self.t = [P.tile(f"{name}{i}", shape, dt) for i in range(n)]
        self.b = [Buf() for _ in range(n)]
        self.i = 0
        self.n = n

    def next(self):
        k = self.i % self.n
        self.i += 1
        return self.t[k], self.b[k]


def build(depth=4, dbg=None, layers=None):
    nc = bass.Bass("TRN2", target_bir_lowering=False)

    def din(name, shape, dt=F32):
        return nc.dram_tensor(name, list(shape), dt, kind="ExternalInput").ap()

    def dscr(name, shape, dt):
        return nc.dram_tensor(name, list(shape), dt, kind="Internal").ap()

    xin = din("xin", [D, TT])
    cT = din("cT", [128, KC, 3])
    w_mod = din("w_mod", [4, 96, 128, D])
    b_mod = din("b_mod", [128, 4, 96])
    w_gu = din("w_gu", [4, 88, 128, D])
    w_down = din("w_down", [4, 16, 128, FF])
    n1g = din("n1g", [128, 4, KC])
    n2g = din("n2g", [128, 4, KC])
    fing = din("fing", [128, KC])
    c_pw1 = din("c_pw1", [2, 32, 128, D])
    c_bpw1 = din("c_bpw1", [128, 2, 32])
    c_wdw = din("c_wdw", [128, 2, KC, 31])
    c_bdw = din("c_bdw", [128, 2, KC])
    c_lng = din("c_lng", [128, 2, KC])
    c_lnb = din("c_lnb", [128, 2, KC])
    c_pw2 = din("c_pw2", [2, 16, 128, D])
    c_bpw2 = din("c_bpw2", [128, 2, KC])
    m_win = din("m_win", [2, 48, 128, D])
    m_wg = din("m_wg", [128, 2, KC, 16])
    m_bg = din("m_bg", [4, 2, 4])
    m_hng = din("m_hng", [128, 2, KC])
    m_wout = din("m_wout", [2, 16, 128, D])
    identd = din("ident", [128, 128])
    masksd = din("masks", [128, 8, 512])
    seld = din("sel", [4, 4, 128])

    out = nc.dram_tensor("out", [D, 4096], F32, kind="ExternalOutput").ap()

    XR = [dscr("xr0", [D, TT], F32), dscr("xr1", [D, TT], F32)]
    HB = dscr("hb", [D, TT], BF16)
    HB2 = dscr("hb2", [D, TT], BF16)
    UG = dscr("ug", [D, TT], F32)
    VC = dscr("vc", [D, TT], F32)
    HID = dscr("hid", [FF, TT], BF16)
    QD = dscr("qd", [1024, TT], BF16)
    KD = dscr("kd", [1024, TT], BF16)
    VD = dscr("vd", [TT, D], BF16)
    SO = dscr("so", [D, TT], BF16)
    GD = dscr("gd", [4, 4, TT], F32)

    P = Prog(nc, strict=STRICT)
    MOD = P.gtile("mod", [128, 4, 96, 3], F32)
    A1 = P.gtile("a1", [128, 4, KC, 3], F32)
    A2 = P.gtile("a2", [128, 4, KC, 3], F32)
    GB1 = P.gtile("gb1", [128, 4, KC, 3], F32)
    n1t = P.gtile("n1t", [128, 4, KC], F32)
    n2t = P.gtile("n2t", [128, 4, KC], F32)
    fint = P.gtile("fint", [128, KC], F32)
    bmt = P.gtile("bmt", [128, 4, 96], F32)
    cb1 = P.gtile("cb1", [128, 2, 32], F32)
    cbd = P.gtile("cbd", [128, 2, KC], F32)
    clg = P.gtile("clg", [128, 2, KC], F32)
    clb = P.gtile("clb", [128, 2, KC], F32)
    cb2 = P.gtile("cb2", [128, 2, KC], F32)
    mhg = P.gtile("mhg", [128, 2, KC], F32)
    mbg = P.gtile("mbg", [4, 2, 4], F32)
    ident = P.gtile("ident", [128, 128], F32)
    ones_b = P.gtile("ones_b", [128, 128], BF16)
    CONST = Buf()

    def MODv(l, part, kc, g):
        return MOD[:, l, part * 16 + kc, g:g + 1]

    csg = P.gtile("csg", [128, KC, 3], BF16)

    def mod_gen(l, wr, ps, pb):
        def issue(n4):
            wt, wb = wr.next()
            P.dma("pool", wt[:], w_mod[l, n4 * 4:(n4 + 1) * 4].rearrange("n p k -> p n k"), writes=[wb], sem_buf=wb)
            return wt, wb
        nxt_ = issue(0)
        for n4 in range(24):
            wt, wb = nxt_
            if n4 + 1 < 24:
                nxt_ = issue(n4 + 1)
            for ci in range(4):
                n = n4 * 4 + ci
                for kc in range(KC):
                    P.op("pe", lambda e, wt=wt, ci=ci, kc=kc, n=n: e.matmul(
                        ps[:, n * 3:(n + 1) * 3], lhsT=wt[:, ci, kc * 128:(kc + 1) * 128], rhs=csg[:, kc, :],
                        start=(kc == 0), stop=(kc == KC - 1)),
                        reads=[wb, CONST], writes=[pb], signal=(kc == KC - 1))
            yield
        for g in range(3):
            P.op("dve", lambda e, g=g: e.tensor_tensor(
                out=MOD[:, l, :, g], in0=ps[:, 0:288].rearrange("p (n g) -> p n g", g=3)[:, :, g], in1=bmt[:, l, :], op=ALU.add),
                reads=[pb, CONST], writes=[CONST])
        for g in range(3):
            P.op("dve", lambda e, g=g: e.scalar_tensor_tensor(
                out=A1[:, l, :, g], in0=MOD[:, l, 16:32, g], scalar=1.0, in1=n1t[:, l, :],
                op0=ALU.add, op1=ALU.mult), reads=[CONST], writes=[CONST])
            P.op("dve", lambda e, g=g: e.scalar_tensor_tensor(
                out=A2[:, l, :, g], in0=MOD[:, l, 64:80, g], scalar=1.0, in1=n2t[:, l, :],
                op0=ALU.add, op1=ALU.mult), reads=[CONST], writes=[CONST])
            if l % 2 == 0:
                P.op("dve", lambda e, g=g: e.tensor_tensor(
                    out=GB1[:, l, :, g], in0=MOD[:, l, 32:48, g], in1=cb2[:, l // 2, :], op=ALU.mult),
                    reads=[CONST], writes=[CONST])
        mod_done.add(l)
        yield

    mod_done = set()

    def mod_start(nslots=2):
        todo = [l for l in (layers if layers is not None else range(depth)) if l not in mod_done and l not in mod_started]
        if not todo:
            return None
        l = todo[0]
        mod_started.add(l)
        wr = Ring(P, "wm", nslots, [128, 4, D], BF16)
        return mod_gen(l, wr, P.psum("mps"), Buf())

    mod_started = set()

    def phase_mod():
        for t, d in ((n1t, n1g), (n2t, n2g), (fint, fing), (bmt, b_mod), (cb1, c_bpw1),
                     (cbd, c_bdw), (clg, c_lng), (clb, c_lnb), (cb2, c_bpw2), (mhg, m_hng), (mbg, m_bg),
                     (ident, identd)):
            P.dma("sync", t[:], d, writes=[CONST], sem_buf=CONST)
        P.op("dve", lambda e: e.memset(ones_b[:], 1.0), writes=[CONST])
        cf = P.tile("cf", [128, KC, 3], F32)
        cb = Buf()
        P.dma("sync", cf[:], cT, writes=[cb], sem_buf=cb)
        P.op("act", lambda e: e.activation(out=csg[:], in_=cf[:], func=AF.Silu), reads=[cb], writes=[CONST])
        first_layers = [l for l in (layers if layers is not None else range(depth))]
        npre = 1 if (first_layers and first_layers[0] % 2 == 0 and MOD_OVERLAP) else len(first_layers)
        for _ in range(npre):
            gen = mod_start(nslots=5)
            if gen is None:
                break
            for _ in gen:
                pass
        P.end_phase()

    def phase_norm(src, Asc, Bfn, dst, dst_dt, blks=range(NBLK), dst_tok_off=0):
        xts = [P.tile(f"nx{i}", [128, KC, 512], F32) for i in range(2)]
        xbs = [[Buf() for _ in range(KC)] for _ in range(2)]
        sq = P.tile("nsq", [128, KC, 512], BF16)
        sqb = [Buf(), Buf()]
        hr = Ring(P, "nh", 2, [128, KC, 512], dst_dt)
        ps = P.psum("nps")
        pb = Buf()
        rs = P.tile("nrs", [128, 512], F32)
        rb = Buf()
        srcv = src.rearrange("(k p) t -> p k t", p=128)
        dstv = dst.rearrange("(k p) t -> p k t", p=128)
        H = KC // 2
        blks = list(blks)
        P.dma("sync", xts[0][:], srcv[:, :, blks[0] * 512:blks[0] * 512 + 512], writes=xbs[0], sem_buf=xbs[0][0])
        for bi, blk in enumerate(blks):
            g = grp_of_blk(blk)
            t0 = blk * 512
            xt, xb = xts[bi % 2], xbs[bi % 2]
            ht, hb = hr.next()
            if bi + 1 < len(blks):
                nb = blks[bi + 1] * 512
                P.dma("sync", xts[(bi + 1) % 2][:], srcv[:, :, nb:nb + 512], writes=xbs[(bi + 1) % 2], sem_buf=xbs[(bi + 1) % 2][0])
            P.op("pool", lambda e, xt=xt: e.tensor_tensor(out=sq[:, 0:H, :], in0=xt[:, 0:H, :], in1=xt[:, 0:H, :], op=ALU.mult), reads=xb[0:H], writes=[sqb[0]])
            P.op("act", lambda e, xt=xt: e.activation(out=sq[:, H:KC, :], in_=xt[:, H:KC, :], func=AF.Square), reads=xb[H:KC], writes=[sqb[1]])
            for kc in range(KC):
                P.op("pe", lambda e, kc=kc: e.matmul(ps[:], lhsT=ones_b[:], rhs=sq[:, kc, :], start=(kc == 0), stop=(kc == KC - 1)),
                     reads=[sqb[kc // H], CONST], writes=[pb], signal=(kc == KC - 1))
            P.op("act", lambda e: e.activation(out=rs[:], in_=ps[:], func=AF.Sqrt, bias=epst[:, 0:1], scale=1.0 / D), reads=[pb, CONST], writes=[rb])
            P.op("dve", lambda e: e.reciprocal(out=rs[:], in_=rs[:]), reads=[rb], writes=[rb])
            for kc in range(KC):
                P.op("dve", lambda e, xt=xt, kc=kc, g=g: e.scalar_tensor_tensor(
                    out=xt[:, kc, :], in0=xt[:, kc, :], scalar=Asc(kc, g), in1=rs[:], op0=ALU.mult, op1=ALU.mult),
                    reads=[xb[kc], rb, CONST], writes=[xb[kc]])
                bb = Bfn(kc, g)
                if bb is None:
                    P.op("act", lambda e, xt=xt, ht=ht, kc=kc: e.activation(out=ht[:, kc, :], in_=xt[:, kc, :], func=AF.Identity),
                         reads=[xb[kc]], writes=[hb])
                else:
                    P.op("act", lambda e, xt=xt, ht=ht, kc=kc, bb=bb: e.activation(out=ht[:, kc, :], in_=xt[:, kc, :], func=AF.Identity, bias=bb, scale=1.0),
                         reads=[xb[kc], CONST], writes=[hb])
            P.dma("sync", dstv[:, :, t0 - dst_tok_off:t0 - dst_tok_off + 512], ht[:], reads=[hb], sem_buf=hb)
        P.end_phase()

    def grp_of_tok(t):
        return 2 if t < 512 else (0 if t < 2560 else 1)

    class NormConsumer:
        def __init__(self, src, Asc, Bfn, dst, dst_dt, dst_tok_off, T):
            self.src, self.Asc, self.Bfn, self.dst_dt, self.off, self.T = src, Asc, Bfn, dst_dt, dst_tok_off, T
            self.xt = P.tile("kx", [128, KC, T], F32)
            self.xb = [Buf() for _ in range(KC)]
            self.sq = P.tile("ksq", [128, KC, T], BF16)
            self.sqb = [Buf(), Buf()]
            self.inplace = (dst_dt == F32)
            self.ht = self.xt if self.inplace else P.tile("kh", [128, KC, T], dst_dt)
            self.hb = Buf()
            self.ps = P.psum("kps")
            self.pb = Buf()
            self.rs = P.tile("krs", [128, T], F32)
            self.rb = Buf()
            self.srcv = src.rearrange("(k p) t -> p k t", p=128)
            self.dstv = dst.rearrange("(k p) t -> p k t", p=128)
            self.q = []

        def add(self, tok0, ntok, dep):
            T, xt, xb, sq, sqb, ht, hb, ps, pb, rs, rb = self.T, self.xt, self.xb, self.sq, self.sqb, self.ht, self.hb, self.ps, self.pb, self.rs, self.rb
            H = KC // 2
            Asc, Bfn = self.Asc, self.Bfn
            for t0 in range(tok0, tok0 + ntok, T):
                g = grp_of_tok(t0)

                def s1(t0=t0):
                    P.dma("sync", xt[:], self.srcv[:, :, t0:t0 + T], reads=[dep], writes=xb, sem_buf=xb[0])

                def s2():
                    P.op("pool", lambda e: e.tensor_tensor(out=sq[:, 0:H, :], in0=xt[:, 0:H, :], in1=xt[:, 0:H, :], op=ALU.mult), reads=xb[0:H], writes=[sqb[0]])
                    P.op("act", lambda e: e.activation(out=sq[:, H:KC, :], in_=xt[:, H:KC, :], func=AF.Square), reads=xb[H:KC], writes=[sqb[1]])

                def s3():
                    for kc in range(KC):
                        P.op("pe", lambda e, kc=kc: e.matmul(ps[:, 0:T], lhsT=ones_b[:], rhs=sq[:, kc, :], start=(kc == 0), stop=(kc == KC - 1)),
                             reads=[sqb[kc // H], CONST], writes=[pb], signal=(kc == KC - 1))

                def s4():
                    P.op("act", lambda e: e.activation(out=rs[:], in_=ps[:, 0:T], func=AF.Sqrt, bias=epst[:, 0:1], scale=1.0 / D), reads=[pb, CONST], writes=[rb])
                    P.op("dve", lambda e: e.reciprocal(out=rs[:], in_=rs[:]), reads=[rb], writes=[rb])

                def s5(g=g, lo=0, hi=H):
                    for kc in range(lo, hi):
                        P.op("dve", lambda e, kc=kc: e.scalar_tensor_tensor(
                            out=xt[:, kc, :], in0=xt[:, kc, :], scalar=Asc(kc, g), in1=rs[:], op0=ALU.mult, op1=ALU.mult),
                            reads=[xb[kc], rb, CONST], writes=[xb[kc]])
                        bb = Bfn(kc, g)
                        if bb is None:
                            P.op("act", lambda e, kc=kc: e.activation(out=ht[:, kc, :], in_=xt[:, kc, :], func=AF.Identity),
                                 reads=[xb[kc]], writes=[hb])
                        else:
                            P.op("act", lambda e, kc=kc, bb=bb: e.activation(out=ht[:, kc, :], in_=xt[:, kc, :], func=AF.Identity, bias=bb, scale=1.0),
                                 reads=[xb[kc], CONST], writes=[hb])

                def s5b(g=g):
                    s5(g, H, KC)

                def s6(t0=t0):
                    P.dma("sync", self.dstv[:, :, t0 - self.off:t0 - self.off + T], ht[:], reads=([hb] + (list(xb) if self.inplace else [])), sem_buf=hb)
                nop = (lambda: None)
                self.q.extend([s1, nop, nop, s2, nop, s3, s4, nop, s5, s5b, s6])

        def pop(self):
            if self.q:
                self.q.pop(0)()

        def flush(self):
            while self.q:
                self.q.pop(0)()

    def gemm(src, kcn, sbs, units, wsrc, epi, umax, extra=None, nw=3, pre=None, nin=2, consumer=None, npb=2):
        sbmax = max(n for _, n in sbs) * 512
        intiles = [P.tile(f"gin{i}", [128, kcn, sbmax], BF16) for i in range(nin)]
        inbs = [Buf() for _ in range(nin)]
        wr = Ring(P, "gw", nw, [128, umax, kcn * 128], BF16)
        ps = [[P.psum("gps") for _ in range(umax if umax <= 2 else 1)] for _ in range(npb)]
        psb = [[Buf() for _ in p] for p in ps]
        gi = 0
        order = [(unit, t0_ + tb_ * 512) for (t0_, nb_) in sbs for unit in units for tb_ in range(nb_)]
        st = (pre(order) if getattr(pre, "wants_order", False) else pre()) if pre is not None else None
        cons = consumer() if consumer is not None else None
        sb_dep = [Buf() for _ in sbs]

        def load_in(i):
            tok0, nblk = sbs[i]
            it, ib = intiles[i % nin], inbs[i % nin]
            for kc in range(kcn):
                P.dma("sync", it[:, kc, 0:nblk * 512], src[kc * 128:(kc + 1) * 128, tok0:tok0 + nblk * 512], writes=[ib], sem_buf=ib)

        load_in(0)
        for sbi, (tok0, nblk) in enumerate(sbs):
            if sbi + 1 < len(sbs) and nin > 1:
                load_in(sbi + 1)
            intile, inb = intiles[sbi % nin], inbs[sbi % nin]
            if isinstance(st, dict):
                st["dep"] = sb_dep[sbi]
            for ui, unit in enumerate(units):
                wt, wb = wr.next()
                for ci, ch in enumerate(unit):
                    P.dma("pool", wt[:, ci, :], wsrc(ch), writes=[wb], sem_buf=wb)
                for tb in range(nblk):
                    s = gi % npb
                    gi += 1
                    for ci in range(len(unit)):
                        for kc in range(kcn):
                            P.op("pe", lambda e, p=ps[s][ci], wt=wt, ci=ci, kc=kc, tb=tb, intile=intile: e.matmul(
                                p[:], lhsT=wt[:, ci, kc * 128:(kc + 1) * 128], rhs=intile[:, kc, tb * 512:(tb + 1) * 512],
                                start=(kc == 0), stop=(kc == kcn - 1)),
                                reads=[wb, inb], writes=[psb[s][ci]], signal=(kc == kcn - 1))
                    epi(ui, unit, tok0 + tb * 512, [ps[s][ci] for ci in range(len(unit))], [psb[s][ci] for ci in range(len(unit))], st)
                    if cons is not None:
                        cons.pop()
            if extra is not None:
                extra(tok0, nblk, intile, inb, wr, ps, psb, st)
            if nin == 1 and sbi + 1 < len(sbs):
                load_in(sbi + 1)
            if cons is not None:
                cons.add(tok0, nblk * 512, sb_dep[sbi])
        if cons is not None:
            cons.flush()
        P.end_phase()

    SB16 = [(0, 3), (1536, 3), (3072, 3)]
    SB44 = [(0, 2), (1024, 2), (2048, 2), (3072, 2), (4096, 1)]
    SB16L = [(512, 3), (2048, 3), (3584, 2)]
    SB44L = [(512, 2), (1536, 2), (2560, 2), (3584, 2)]

    def resid_epi(xsrc, xdst, gate_fn, gb_fn):
        AHEAD = 2

        def pre(order=None):
            return {0: Ring(P, "rx", 4, [128, 512], F32), 1: Ring(P, "ro", 3, [128, 512], F32), "dep": None,
                    "order": order, "idx": 0, "pend": []}
        pre.wants_order = True

        def issue(st, k):
            unit_, tok_ = st["order"][k]
            j_ = unit_[0]
            xo, xob = st[0].next()
            P.dma("sync", xo[:], xsrc[j_ * 128:(j_ + 1) * 128, tok_:tok_ + 512], writes=[xob], sem_buf=xob)
            st["pend"].append((xo, xob))

        def epi(ui, unit, tok, pss, psbs, st):
            j = unit[0]
            g = grp_of_blk(tok // 512)
            n_ = len(st["order"])
            if st["idx"] == 0:
                for k in range(min(AHEAD, n_)):
                    issue(st, k)
            assert st["order"][st["idx"]] == (unit, tok)
            xo, xob = st["pend"].pop(0)
            if st["idx"] + AHEAD < n_:
                issue(st, st["idx"] + AHEAD)
            st["idx"] += 1
            oo, oob = st[1].next()
            if gb_fn is not None:
                P.op("act", lambda e, xo=xo, j=j, g=g: e.activation(out=xo[:], in_=xo[:], func=AF.Identity, bias=gb_fn(j, g), scale=1.0),
                     reads=[xob, CONST], writes=[xob])
            P.op("dve", lambda e, xo=xo, oo=oo, p=pss[0], j=j, g=g: e.scalar_tensor_tensor(
                out=oo[:], in0=p[:], scalar=gate_fn(j, g), in1=xo[:], op0=ALU.mult, op1=ALU.add),
                reads=[psbs[0], xob, CONST], writes=[oob])
            P.dma("sync", xdst[j * 128:(j + 1) * 128, tok:tok + 512], oo[:], reads=[oob], marks=([st["dep"]] if st["dep"] is not None else []), sem_buf=oob)
        return pre, epi

    def phase_ffn(l, xs, xd, last=False, consumer=None):
        sbs = SB44L if last else SB44
        cons = consumer() if consumer is not None else None
        sb_dep = [Buf() for _ in sbs]
        xin_t = P.tile("fx", [128, KC, 1024], BF16)
        xinb = Buf()
        hid = P.tile("fh", [128, FC, 1024], BF16)
        hidb = Buf()
        gur = Ring(P, "fgw", 3, [128, 2, D], BF16)
        dwr = Ring(P, "fdw", 2, [128, FF], BF16)
        sgr = Ring(P, "fs", 2, [128, 512], F32)
        rxr = Ring(P, "frx", 2, [128, 512], F32)
        ror = Ring(P, "fro", 2, [128, 512], F32)
        gps = [[P.psum("fgp") for _ in range(2)] for _ in range(2)]
        gpb = [[Buf() for _ in range(2)] for _ in range(2)]
        dps = [P.psum("fdp") for _ in range(3)]
        dpb = [Buf() for _ in range(3)]
        gi = 0
        di = 0

        def load_x(i):
            tok0, nblk = sbs[i]
            for kc in range(KC):
                P.dma("sync", xin_t[:, kc, 0:nblk * 512], HB2[kc * 128:(kc + 1) * 128, tok0:tok0 + nblk * 512], writes=[xinb], sem_buf=xinb)

        load_x(0)
        for sbi, (tok0, nblk) in enumerate(sbs):
            for j in range(FC):
                wt, wb = gur.next()
                P.dma("pool", wt[:, 0, :], w_gu[l, j], writes=[wb], sem_buf=wb)
                P.dma("pool", wt[:, 1, :], w_gu[l, FC + j], writes=[wb], sem_buf=wb)
                for tb in range(nblk):
                    s_ = gi % 2
                    gi += 1
                    for ci in range(2):
                        for kc in range(KC):
                            P.op("pe", lambda e, p=gps[s_][ci], wt=wt, ci=ci, kc=kc, tb=tb: e.matmul(
                                p[:], lhsT=wt[:, ci, kc * 128:(kc + 1) * 128], rhs=xin_t[:, kc, tb * 512:(tb + 1) * 512],
                                start=(kc == 0), stop=(kc == KC - 1)),
                                reads=[wb, xinb], writes=[gpb[s_][ci]], signal=(kc == KC - 1))
                    sg, sgb = sgr.next()
                    P.op("act", lambda e, sg=sg, p=gps[s_][0]: e.activation(out=sg[:], in_=p[:], func=AF.Silu), reads=[gpb[s_][0]], writes=[sgb])
                    P.op("dve", lambda e, sg=sg, p=gps[s_][1], j=j, tb=tb: e.tensor_tensor(out=hid[:, j, tb * 512:(tb + 1) * 512], in0=p[:], in1=sg[:], op=ALU.mult),
                         reads=[gpb[s_][1], sgb], writes=[hidb])
                    if cons is not None:
                        cons.pop()
            if sbi + 1 < len(sbs):
                load_x(sbi + 1)
            dgroups = [(j_, tok0 + tb_ * 512) for j_ in range(KC) for tb_ in range(nblk)]
            pend = []

            def issue_x(k):
                j_, tok_ = dgroups[k]
                xo_, xob_ = rxr.next()
                P.dma("sync", xo_[:], xs[j_ * 128:(j_ + 1) * 128, tok_:tok_ + 512], writes=[xob_], sem_buf=xob_)
                pend.append((xo_, xob_))
            issue_x(0)
            dk = 0
            for j in range(KC):
                wt, wb = dwr.next()
                P.dma("pool", wt[:], w_down[l, j], writes=[wb], sem_buf=wb)
                for tb in range(nblk):
                    s_ = di % 3
                    di += 1
                    tok = tok0 + tb * 512
                    g = grp_of_blk(tok // 512)
                    xo, xob = pend.pop(0)
                    dk += 1
                    if dk < len(dgroups):
                        issue_x(dk)
                    oo, oob = ror.next()
                    for kc in range(FC):
                        P.op("pe", lambda e, p=dps[s_], wt=wt, kc=kc, tb=tb: e.matmul(
                            p[:], lhsT=wt[:, kc * 128:(kc + 1) * 128], rhs=hid[:, kc, tb * 512:(tb + 1) * 512],
                            start=(kc == 0), stop=(kc == FC - 1)),
                            reads=[wb, hidb], writes=[dpb[s_]], signal=(kc == FC - 1))
                    P.op("dve", lambda e, xo=xo, oo=oo, p=dps[s_], j=j, g=g: e.scalar_tensor_tensor(
                        out=oo[:], in0=p[:], scalar=MODv(l, 5, j, g), in1=xo[:], op0=ALU.mult, op1=ALU.add),
                        reads=[dpb[s_], xob, CONST], writes=[oob])
                    P.dma("sync", xd[j * 128:(j + 1) * 128, tok:tok + 512], oo[:], reads=[oob], marks=[sb_dep[sbi]], sem_buf=oob)
                    if cons is not None:
                        cons.pop()
            if cons is not None:
                cons.add(tok0, nblk * 512, sb_dep[sbi])
        if cons is not None:
            cons.flush()
        P.end_phase()

    def phase_conv(l, xs, xd, last=False, consumer=None):
        jl = l // 2

        def pre():
            return (Ring(P, "cs", 2, [128, 512], F32), Ring(P, "co", 3, [128, 512], F32))

        def epi(ui, unit, tok, pss, psbs, st):
            j = unit[0]
            sg, sgb = st[0].next()
            oo, oob = st[1].next()
            P.op("act", lambda e, sg=sg, p=pss[1], j=j: e.activation(out=sg[:], in_=p[:], func=AF.Sigmoid, bias=cb1[:, jl, 16 + j:17 + j], scale=1.0),
                 reads=[psbs[1], CONST], writes=[sgb])
            P.op("dve", lambda e, sg=sg, oo=oo, p=pss[0], j=j: e.scalar_tensor_tensor(
                out=oo[:], in0=p[:], scalar=cb1[:, jl, j:j + 1], in1=sg[:], op0=ALU.add, op1=ALU.mult),
                reads=[psbs[0], sgb, CONST], writes=[oob])
            P.dma("sync", UG[j * 128:(j + 1) * 128, tok:tok + 512], oo[:], reads=[oob], sem_buf=oob)
        gemm(HB, KC, SB16, [(j, 16 + j) for j in range(KC)], lambda ch: c_pw1[jl, ch], epi, 2, pre=pre, npb=3)

        ur = Ring(P, "cu", 2, [128, TT], F32)
        vr = Ring(P, "cv", 2, [128, TT], F32)
        pcs = [P.tile(f"cpc{i}", [128, 2, 286], BF16) for i in range(2)]
        phs = [P.tile(f"cph{i}", [128, 2, 32, 94], BF16) for i in range(2)]
        pvs = [P.tile(f"cpv{i}", [128, 2, 62, 64], BF16) for i in range(2)]
        pbs = [Buf(), Buf()]
        dgr = Ring(P, "cdg", 2, [128, 31, 128], BF16)
        identb = P.tile("cidb", [128, 128], BF16)
        idb = Buf()
        cwd = P.tile("ccwd", [128, 2, KC, 31], F32)
        P.dma("sync", cwd[:], c_wdw, writes=[idb], sem_buf=idb)
        cps = [P.psum("cps") for _ in range(3)]
        cpb = [Buf() for _ in range(3)]
        P.op("dve", lambda e: e.tensor_copy(out=identb[:], in_=ident[:]), reads=[CONST], writes=[idb])
        for i in range(2):
            P.op("dve", lambda e, i=i: e.memset(pcs[i][:], 0.0), writes=[pbs[i]])
            P.op("dve", lambda e, i=i: e.memset(phs[i][:], 0.0), writes=[pbs[i]])
            P.op("dve", lambda e, i=i: e.memset(pvs[i][:], 0.0), writes=[pbs[i]])
        cgi = 0
        prepd = {}

        def conv_prep(j):
            ut, ub = ur.next()
            pc, ph, pv, pbuf = pcs[j % 2], phs[j % 2], pvs[j % 2], pbs[j % 2]
            dg, dgb = dgr.next()
            P.dma("sync", ut[:], UG[j * 128:(j + 1) * 128, :], writes=[ub], sem_buf=ub)
            uc = ut[:, 0:512].rearrange("p (b t) -> p b t", b=2)
            ux = ut[:, 512:TT].rearrange("p (b r c) -> p b r c", b=2, r=32)
            horiz = j < 8
            P.op("act", lambda e, uc=uc, pc=pc: e.activation(out=pc[:, :, 15:271], in_=uc, func=AF.Identity), reads=[ub], writes=[pbuf])
            for b in range(2):
                if horiz:
                    P.op("act", lambda e, ux=ux, b=b, ph=ph: e.activation(out=ph[:, b, :, 15:79], in_=ux[:, b], func=AF.Identity), reads=[ub], writes=[pbuf])
                else:
                    P.op("act", lambda e, ux=ux, b=b, pv=pv: e.activation(out=pv[:, b, 15:47, :], in_=ux[:, b], func=AF.Identity), reads=[ub], writes=[pbuf])
            for k in range(31):
                P.op("pool", lambda e, dg=dg, k=k, j=j: e.tensor_scalar(out=dg[:, k, :], in0=identb[:], scalar1=cwd[:, jl, j, k:k + 1], scalar2=0.0,
                                                                    op0=ALU.mult, op1=ALU.add), reads=[idb, CONST], writes=[dgb])
            prepd[j] = (pc, ph, pv, pbuf, dg, dgb, horiz)

        mgen = mod_start() if MOD_OVERLAP else None
        conv_prep(0)
        for j in range(KC):
            if j + 1 < KC:
                conv_prep(j + 1)
            if mgen is not None:
                for _ in range(2):
                    if next(mgen, "end") == "end":
                        mgen = None
                        break
            pc, ph, pv, pbuf, dg, dgb, horiz = prepd.pop(j)
            vt, vb = vr.next()
            for blk in range(NBLK):
                ci_ = cgi % 3
                cgi += 1
                ps_, psb_ = cps[ci_], cpb[ci_]
                for k in range(31):
                    if blk == 0:
                        rhs = pc[:, 0:2, k:k + 256]
                        o_ = ps_[:].rearrange("p (b t) -> p b t", b=2)
                    else:
                        b = (blk - 1) // 4
                        r0 = ((blk - 1) % 4) * 8
                        o_ = ps_[:].rearrange("p (r c) -> p r c", r=8)
                        if horiz:
                            rhs = ph[:, b, r0:r0 + 8, k:k + 64]
                        else:
                            rhs = pv[:, b, r0 + k:r0 + k + 8, :]
                    P.op("pe", lambda e, o_=o_, rhs=rhs, dg=dg, k=k: e.matmul(o_, lhsT=dg[:, k, :], rhs=rhs, start=(k == 0), stop=(k == 30)),
                         reads=[dgb, pbuf], writes=[psb_], signal=(k == 30))
                P.op("act", lambda e, vt=vt, ps_=ps_, blk=blk, j=j: e.activation(out=vt[:, blk * 512:(blk + 1) * 512], in_=ps_[:], func=AF.Identity,
                                                                            bias=cbd[:, jl, j:j + 1], scale=1.0),
                     reads=[psb_, CONST], writes=[vb])
            P.dma("sync", VC[j * 128:(j + 1) * 128, :], vt[:], reads=[vb], sem_buf=vb)
        if mgen is not None:
            for _ in mgen:
                pass
        P.end_phase()

        xts = [P.tile(f"lx{i}", [128, KC, 512], F32) for i in range(2)]
        xbs = [[Buf() for _ in range(KC)] for _ in range(2)]
        vb16 = P.tile("lvb", [128, KC, 512], BF16)
        sq = P.tile("lsq", [128, KC, 512], BF16)
        vbb = Buf()
        sqb = Buf()
        hr = Ring(P, "lh", 2, [128, KC, 512], BF16)
        ps1 = P.psum("lps1")
        ps2 = P.psum("lps2")
        pb1 = Buf()
        pb2 = Buf()
        mean = P.tile("lmean", [128, 512], F32)
        rs = P.tile("lrs", [128, 512], F32)
        tmp = P.tile("ltmp", [128, 512], F32)
        rb = Buf()
        srcv = VC.rearrange("(k p) t -> p k t", p=128)
        dstv = HB.rearrange("(k p) t -> p k t", p=128)
        P.dma("sync", xts[0][:], srcv[:, :, 0:512], writes=xbs[0], sem_buf=xbs[0][0])
        mgen2 = mod_start() if MOD_OVERLAP else None
        for blk in range(NBLK):
            t0 = blk * 512
            xt, xb = xts[blk % 2], xbs[blk % 2]
            ht, hb = hr.next()
            if blk + 1 < NBLK:
                P.dma("sync", xts[(blk + 1) % 2][:], srcv[:, :, t0 + 512:t0 + 1024], writes=xbs[(blk + 1) % 2], sem_buf=xbs[(blk + 1) % 2][0])
            if mgen2 is not None:
                for _ in range(3):
                    if next(mgen2, "end") == "end":
                        mgen2 = None
                        break
            P.op("act", lambda e, xt=xt: e.activation(out=vb16[:], in_=xt[:], func=AF.Identity), reads=xb, writes=[vbb])
            P.op("pool", lambda e, xt=xt: e.tensor_tensor(out=sq[:, 0:10, :], in0=xt[:, 0:10, :], in1=xt[:, 0:10, :], op=ALU.mult), reads=xb, writes=[sqb])
            P.op("act", lambda e, xt=xt: e.activation(out=sq[:, 10:KC, :], in_=xt[:, 10:KC, :], func=AF.Square), reads=xb, writes=[sqb])
            for kc in range(KC):
                P.op("pe", lambda e, kc=kc: e.matmul(ps1[:], lhsT=ones_b[:], rhs=vb16[:, kc, :], start=(kc == 0), stop=(kc == KC - 1)),
                     reads=[vbb, CONST], writes=[pb1], signal=(kc == KC - 1))
            P.op("act", lambda e: e.activation(out=mean[:], in_=ps1[:], func=AF.Identity, scale=1.0 / D), reads=[pb1], writes=[rb])
            P.op("dve", lambda e: e.tensor_tensor(out=tmp[:], in0=mean[:], in1=mean[:], op=ALU.mult), reads=[rb], writes=[rb])
            for kc in range(KC):
                P.op("pe", lambda e, kc=kc: e.matmul(ps2[:], lhsT=ones_b[:], rhs=sq[:, kc, :], start=(kc == 0), stop=(kc == KC - 1)),
                     reads=[sqb, CONST], writes=[pb2], signal=(kc == KC - 1))
            P.op("dve", lambda e: e.scalar_tensor_tensor(out=rs[:], in0=ps2[:], scalar=1.0 / D, in1=tmp[:], op0=ALU.mult, op1=ALU.subtract),
                 reads=[pb2, rb], writes=[rb])
            P.op("act", lambda e: e.activation(out=rs[:], in_=rs[:], func=AF.Sqrt, bias=epst[:, 0:1], scale=1.0), reads=[rb, CONST], writes=[rb])
            P.op("dve", lambda e: e.reciprocal(out=rs[:], in_=rs[:]), reads=[rb], writes=[rb])
            for kc in range(KC):
                P.op("dve", lambda e, xt=xt, kc=kc: e.tensor_tensor(out=xt[:, kc, :], in0=xt[:, kc, :], in1=mean[:], op=ALU.subtract),
                     reads=[xb[kc], rb], writes=[xb[kc]])
                P.op("pool" if kc % 2 == 0 else "dve", lambda e, xt=xt, kc=kc: e.tensor_tensor(out=xt[:, kc, :], in0=xt[:, kc, :], in1=rs[:], op=ALU.mult),
                     reads=[xb[kc], rb], writes=[xb[kc]])
                P.op("act", lambda e, xt=xt, ht=ht, kc=kc: e.activation(out=ht[:, kc, :], in_=xt[:, kc, :], func=AF.Silu,
                                                                       bias=clb[:, jl, kc:kc + 1], scale=clg[:, jl, kc:kc + 1]),
                     reads=[xb[kc], CONST], writes=[hb])
            P.dma("sync", dstv[:, :, t0:t0 + 512], ht[:], reads=[hb], sem_buf=hb)
        if mgen2 is not None:
            for _ in mgen2:
                pass
        P.end_phase()

        pre2, epi2 = resid_epi(xs, xd, lambda j, g: MODv(l, 2, j, g), lambda j, g: GB1[:, l, j, g:g + 1])
        gemm(HB, KC, SB16L if last else SB16, [(j,) for j in range(KC)], lambda ch: c_pw2[jl, ch], epi2, 1, pre=pre2, consumer=consumer, npb=4)

    def phase_mlstm(l, xs, xd, last=False, consumer=None):
        jl = l // 2

        def pre():
            wg = P.tile("mwg", [128, KC, 16], BF16)
            wgb = Buf()
            P.dma("pool", wg[:], m_wg[:, jl], writes=[wgb], sem_buf=wgb)
            return (Ring(P, "mo", 3, [128, 512], BF16), wg, wgb, Ring(P, "mg", 2, [4, 512], F32))

        def epi(ui, unit, tok, pss, psbs, st):
            ch = unit[0]
            oo, oob = st[0].next()
            if ch < 8:
                dstap = QD[ch * 128:(ch + 1) * 128, tok:tok + 512]
                P.op("act", lambda e, oo=oo, p=pss[0]: e.activation(out=oo[:], in_=p[:], func=AF.Identity), reads=[psbs[0]], writes=[oob])
            elif ch < 16:
                dstap = KD[(ch - 8) * 128:(ch - 7) * 128, tok:tok + 512]
                P.op("act", lambda e, oo=oo, p=pss[0]: e.activation(out=oo[:], in_=p[:], func=AF.Identity, scale=1.0 / 16.0), reads=[psbs[0]], writes=[oob])
            else:
                dstap = SO[(ch - 32) * 128:(ch - 31) * 128, tok:tok + 512]
                P.op("act", lambda e, oo=oo, p=pss[0]: e.activation(out=oo[:], in_=p[:], func=AF.Sigmoid), reads=[psbs[0]], writes=[oob])
            P.dma("sync", dstap, oo[:], reads=[oob], sem_buf=oob)

        def extra(tok0, nblk, intile, inb, wr, ps, psb, st):
            oring, wg, wgb, gring = st
            gi = 0
            for vg in range(4):
                wt, wb = wr.next()
                P.dma("pool", wt[:], m_win[jl, 16 + vg * 4:20 + vg * 4].rearrange("n p k -> p n k"), writes=[wb], sem_buf=wb)
                for t128 in range(nblk * 4):
                    s = gi % len(ps)
                    gi += 1
                    p_, pb_ = ps[s][0], psb[s][0]
                    for kc in range(KC):
                        P.op("pe", lambda e, p_=p_, wt=wt, kc=kc, t128=t128: e.matmul(
                            p_[:].rearrange("p (a c) -> p a c", a=4), lhsT=intile[:, kc, t128 * 128:(t128 + 1) * 128], rhs=wt[:, 0:4, kc * 128:(kc + 1) * 128],
                            start=(kc == 0), stop=(kc == KC - 1)), reads=[wb, inb], writes=[pb_], signal=(kc == KC - 1))
                    oo, oob = oring.next()
                    P.op("act", lambda e, oo=oo, p_=p_: e.activation(out=oo[:], in_=p_[:], func=AF.Identity), reads=[pb_], writes=[oob])
                    r0 = tok0 + t128 * 128
                    P.dma("sync", VD[r0:r0 + 128, vg * 512:(vg + 1) * 512], oo[:], reads=[oob], sem_buf=oob)
            for tb in range(nblk):
                for ty in range(4):
                    s = gi % len(ps)
                    gi += 1
                    p_, pb_ = ps[s][0], psb[s][0]
                    for kc in range(KC):
                        P.op("pe", lambda e, p_=p_, kc=kc, tb=tb, ty=ty: e.matmul(
                            p_[0:4, :], lhsT=wg[:, kc, ty * 4:(ty + 1) * 4], rhs=intile[:, kc, tb * 512:(tb + 1) * 512],
                            start=(kc == 0), stop=(kc == KC - 1)), reads=[wgb, inb], writes=[pb_], signal=(kc == KC - 1))
                    go, gob = gring.next()
                    P.op("act", lambda e, go=go, p_=p_: e.activation(out=go[:], in_=p_[0:4, :], func=AF.Identity), reads=[pb_], writes=[gob])
                    P.dma("sync", GD[ty, :, tok0 + tb * 512:tok0 + (tb + 1) * 512], go[:], reads=[gob], sem_buf=gob)

        units = [(ch,) for ch in list(range(16)) + list(range(32, 48))]
        gemm(HB, KC, SB16, units, lambda ch: m_win[jl, ch], epi, 4, extra=extra, pre=pre, npb=4)

        masks = P.tile("amask", [128, 8, 512], F32)
        mkb = Buf()
        P.dma("sync", masks[:], masksd, writes=[mkb], sem_buf=mkb)
        sel = P.tile("asel", [4, 4, 128], F32)
        P.dma("sync", sel[:], seld, writes=[mkb], sem_buf=mkb)
        qt = P.tile("aq", [128, 2, 2304], BF16)
        kt = P.tile("ak", [128, 2, 2304], BF16)
        vt = P.tile("av", [128, 18, 512], BF16)
        qkvb = Buf()
        brow2 = [P.tile(f"abrow{i}", [128, 2304], F32) for i in range(2)]
        brb2 = [Buf(), Buf()]
        ut = P.tile("aut", [128, 18, 8], F32)
        utb = Buf()
        G2 = P.tile("ag2", [4, 3, 2304], F32)
        gtb = Buf()
        bsum = [P.tile(f"ab{i}", [4, 2304], F32) for i in range(2)]
        bsb = Buf()
        bgs = P.tile("abgs", [4, 4], F32)
        er = Ring(P, "ae", 3, [128, 512], F32)
        tr = Ring(P, "at", 2, [128, 512], F32)
        pr = Ring(P, "ap", 3, [128, 512], BF16)
        rdr = Ring(P, "arden", 3, [128, 512], F32)
        nr = Ring(P, "ansb", 4, [128, 4, 512], F32)
        hsq = P.tile("ahsq", [128, 4, 512], BF16)
        hqb = Buf()
        hrs = P.tile("ahrs", [128, 512], F32)
        hrb = Buf()
        sor = Ring(P, "aso", 2, [128, 4, 512], BF16)
        hor = Ring(P, "aho", 2, [128, 4, 512], BF16)
        stp = [P.psum("ast") for _ in range(2)]
        stb = [Buf() for _ in range(2)]
        nump = [P.psum("anum") for _ in range(4)]
        denp = P.psum("aden")
        accb = Buf()
        misc = P.psum("amisc")
        mib = Buf()
        sti = 0
        tail_q = []

        def acquire(ring):
            t_, b_ = ring.next()
            lastref = -1
            for qi, (_, bufs) in enumerate(tail_q):
                if any(x is b_ for x in bufs):
                    lastref = qi
            for _ in range(lastref + 1):
                tail_q.pop(0)[0]()
            return t_, b_
        P.op("dve", lambda e: e.memset(G2[:, 2, :], 1.0), writes=[gtb])

        for b in range(2):
            cx = (b * 256, 256)
            xx = (512 + b * 2048, 2048)
            for d_ in range(2):
                segs = [(0, cx), (256, xx)] if d_ == 0 else [(0, xx), (2048, cx)]
                for sl, ty in ((0, 2 * d_), (1, 2 * d_ + 1)):
                    for (o, (s0, n)) in segs:
                        P.dma("sync", G2[:, sl, o:o + n], GD[ty, :, s0:s0 + n], writes=[gtb], sem_buf=gtb)
                    P.op("dve", lambda e, sl=sl, ty=ty: e.tensor_scalar(out=G2[:, sl, :], in0=G2[:, sl, :], scalar1=mbg[:, jl, ty:ty + 1], scalar2=None, op0=ALU.add),
                         reads=[gtb, CONST], writes=[gtb])
                    P.op("act", lambda e, sl=sl: e.activation(out=G2[:, sl, :], in_=G2[:, sl, :], func=AF.Tanh, scale=1.0 / 15.0), reads=[gtb], writes=[gtb])
                P.op("act", lambda e: e.activation(out=G2[:, 1, :], in_=G2[:, 1, :], func=AF.Sigmoid, scale=15.0), reads=[gtb], writes=[gtb])
                P.op("act", lambda e: e.activation(out=G2[:, 1, :], in_=G2[:, 1, :], func=AF.Ln), reads=[gtb], writes=[gtb])
                P.op("act", lambda e: e.activation(out=G2[:, 0, :], in_=G2[:, 0, :], func=AF.Identity, scale=15.0), reads=[gtb], writes=[gtb])
                P.op("dve", lambda e, d_=d_: e.tensor_tensor_scan(out=bsum[d_][:], data0=G2[:, 2, :], data1=G2[:, 1, :], initial=0.0, op0=ALU.mult, op1=ALU.add),
                     reads=[gtb], writes=[bsb])
                if d_ == 1:
                    P.op("dve", lambda e: e.tensor_copy(out=bgs[:, 0:1], in_=bsum[1][:, 2303:2304]), reads=[bsb], writes=[bsb])
                    P.op("dve", lambda e: e.tensor_tensor(out=bsum[1][:], in0=G2[:, 1, :], in1=bsum[1][:], op=ALU.subtract), reads=[gtb, bsb], writes=[bsb])
                    P.op("dve", lambda e: e.tensor_scalar(out=bsum[1][:], in0=bsum[1][:], scalar1=bgs[:, 0:1], scalar2=None, op0=ALU.add), reads=[bsb], writes=[bsb])
                P.op("dve", lambda e, d_=d_: e.tensor_tensor(out=G2[:, 0, :], in0=G2[:, 0, :], in1=bsum[d_][:], op=ALU.subtract), reads=[gtb, bsb], writes=[gtb])
                for fb in range(18):
                    lb = fb if d_ == 0 else (fb - 2) % 18
                    P.op("pe", lambda e, fb=fb, lb=lb: e.matmul(misc[:, fb * 4:fb * 4 + 4], lhsT=G2[:, 0, lb * 128:(lb + 1) * 128],
                                                          rhs=ident[0:4, 0:4], start=True, stop=True),
                         reads=[gtb, CONST], writes=[mib], signal=(fb == 17))
                P.op("act", lambda e, d_=d_: e.activation(out=ut[:, :, d_ * 4:d_ * 4 + 4], in_=misc[:, 0:72].rearrange("p (f h) -> p f h", h=4), func=AF.Identity),
                     reads=[mib], writes=[utb, mib])
            for h in range(4):
                for (o, (s0, n)) in [(0, cx), (256, xx)]:
                    for dc in range(2):
                        r0 = h * 256 + dc * 128
                        P.dma("sync", qt[:, dc, o:o + n], QD[r0:r0 + 128, s0:s0 + n], writes=[qkvb], sem_buf=qkvb)
                        P.dma("sync", kt[:, dc, o:o + n], KD[r0:r0 + 128, s0:s0 + n], writes=[qkvb], sem_buf=qkvb)
                    P.dma("sync", vt[:, o // 128:(o + n) // 128, :], VD[s0:s0 + n, h * 512:(h + 1) * 512].rearrange("(f p) e -> p f e", p=128),
                          writes=[qkvb], sem_buf=qkvb)
                for d_ in range(2):
                    for c0 in range(0, 2304, 512):
                        cn = min(512, 2304 - c0)
                        P.op("pe", lambda e, d_=d_, c0=c0, cn=cn, h=h: e.matmul(misc[:, 0:cn], lhsT=sel[:, h, :], rhs=bsum[d_][:, c0:c0 + cn], start=True, stop=True),
                             reads=[bsb, mkb], writes=[mib])
                        P.op("act", lambda e, d_=d_, c0=c0, cn=cn: e.activation(out=brow2[d_][:, c0:c0 + cn], in_=misc[:, 0:cn], func=AF.Identity),
                             reads=[mib], writes=[brb2[d_], mib])
                pairs = []
                for ti in range(1 if last else 0, 5):
                    ffb0, ffb1 = (0, 2) if ti == 0 else (2 + 4 * (ti - 1), 6 + 4 * (ti - 1))
                    for d_ in range(2):
                        if d_ == 0:
                            lb0, lb1 = ffb0, ffb1
                            sbl = list(range(0, lb1))
                        else:
                            lb0, lb1 = ((16, 18) if ti == 0 else (4 * (ti - 1), 4 * ti))
                            sbl = list(range(17, lb0 - 1, -1))
                        for si, sb_ in enumerate(sbl):
                            pairs.append(dict(d=d_, ti=ti, lb0=lb0, lb1=lb1, sb=sb_, first=(si == 0), last=(si == len(sbl) - 1), fc0=ffb0 * 128))

                def stageA(p):
                    nonlocal sti
                    d_ = p["d"]
                    lb0, lb1, sb_ = p["lb0"], p["lb1"], p["sb"]
                    tc = (lb1 - lb0) * 128
                    lc0 = lb0 * 128
                    fc0 = p["fc0"]
                    fsb = sb_ if d_ == 0 else (sb_ + 2) % 18
                    diag = lb0 <= sb_ < lb1
                    s_ = sti % 2
                    sti += 1
                    p.update(tc=tc, fsb=fsb, s_=s_)
                    for dc in range(2):
                        P.op("pe", lambda e, s_=s_, dc=dc, fsb=fsb, fc0=fc0, tc=tc: e.matmul(
                            stp[s_][:, 0:tc], lhsT=kt[:, dc, fsb * 128:(fsb + 1) * 128], rhs=qt[:, dc, fc0:fc0 + tc],
                            start=(dc == 0), stop=(dc == 1)), reads=[qkvb], writes=[stb[s_]], signal=(dc == 1))
                    et, eb = er.next()
                    p.update(et=et, eb=eb)
                    ucol = ut[:, fsb, d_ * 4 + h:d_ * 4 + h + 1]
                    br = brow2[d_]
                    if diag:
                        tt_, ttb = tr.next()
                        mk = masks[:, d_ * 4 + (sb_ - lb0), 0:tc]
                        P.op("dve", lambda e, tt_=tt_, ucol=ucol, mk=mk, lc0=lc0, tc=tc, br=br: e.scalar_tensor_tensor(
                            out=tt_[:, 0:tc], in0=br[:, lc0:lc0 + tc], scalar=ucol, in1=mk, op0=ALU.add, op1=ALU.add),
                            reads=[brb2[d_], utb, mkb], writes=[ttb])
                        P.op("act", lambda e, et=et, tt_=tt_, tc=tc: e.activation(out=et[:, 0:tc], in_=tt_[:, 0:tc], func=AF.Exp),
                             reads=[ttb], writes=[eb])
                    else:
                        P.op("act", lambda e, et=et, ucol=ucol, lc0=lc0, tc=tc, br=br: e.activation(
                            out=et[:, 0:tc], in_=br[:, lc0:lc0 + tc], func=AF.Exp, bias=ucol, scale=1.0),
                            reads=[brb2[d_], utb], writes=[eb])

                hf_state = {}

                def stageBC(p):
                    d_, tc, fc0, fsb, s_, et, eb = p["d"], p["tc"], p["fc0"], p["fsb"], p["s_"], p["et"], p["eb"]
                    pt, ptb = pr.next()
                    P.op("dve", lambda e, pt=pt, et=et, s_=s_, tc=tc: e.tensor_tensor(out=pt[:, 0:tc], in0=stp[s_][:, 0:tc], in1=et[:, 0:tc], op=ALU.mult),
                         reads=[stb[s_], eb], writes=[ptb])
                    first, last = p["first"], p["last"]
                    for ec in range(4):
                        P.op("pe", lambda e, ec=ec, pt=pt, fsb=fsb, tc=tc, first=first, last=last: e.matmul(
                            nump[ec][:, 0:tc], lhsT=vt[:, fsb, ec * 128:(ec + 1) * 128], rhs=pt[:, 0:tc], start=first, stop=last),
                            reads=[qkvb, ptb], writes=[accb], signal=False)
                    P.op("pe", lambda e, pt=pt, tc=tc, first=first, last=last: e.matmul(
                        denp[:, 0:tc], lhsT=ones_b[:], rhs=pt[:, 0:tc], start=first, stop=last),
                        reads=[CONST, ptb], writes=[accb], signal=True)
                    if not last:
                        return
                    nsb, nsbb = acquire(nr)
                    rden, rdb = acquire(rdr)
                    P.op("act", lambda e, tc=tc, rden=rden: e.activation(out=rden[:, 0:tc], in_=denp[:, 0:tc], func=AF.Abs),
                         reads=[accb], writes=[rdb])
                    for ec in range(4):
                        P.op("act", lambda e, ec=ec, nsb=nsb, tc=tc: e.activation(out=nsb[:, ec, 0:tc], in_=nump[ec][:, 0:tc], func=AF.Identity),
                             reads=[accb], writes=[nsbb])

                    def s_rden():
                        P.op("dve", lambda e: e.tensor_scalar_max(out=rden[:, 0:tc], in0=rden[:, 0:tc], scalar1=1.0), reads=[rdb], writes=[rdb])
                        P.op("dve", lambda e: e.reciprocal(out=rden[:, 0:tc], in_=rden[:, 0:tc]), reads=[rdb], writes=[rdb])

                    def s_scale():
                        for ec in range(4):
                            P.op("pool", lambda e, ec=ec: e.tensor_tensor(out=nsb[:, ec, 0:tc], in0=nsb[:, ec, 0:tc], in1=rden[:, 0:tc], op=ALU.mult),
                                 reads=[nsbb, rdb], writes=[nsbb])
                    nop_ = (lambda: None, ())
                    tail_q.extend([(s_rden, (rdb,)), nop_, (s_scale, (rdb, nsbb)), nop_, nop_])
                    if d_ == 0:
                        hf_state["f"] = (nsb, nsbb)
                        return
                    hf, hfb = hf_state.pop("f")
                    ctok = (b * 256) if p["ti"] == 0 else (512 + b * 2048 + (p["ti"] - 1) * 512)
                    hh = h
                    st_ = {}

                    def s_add():
                        for ec in range(4):
                            P.op("pool", lambda e, ec=ec: e.tensor_tensor(out=hf[:, ec, 0:tc], in0=hf[:, ec, 0:tc], in1=nsb[:, ec, 0:tc], op=ALU.add),
                                 reads=[nsbb, hfb], writes=[hfb])

                    def s_sq():
                        st_["so"] = acquire(sor)
                        sot, sob = st_["so"]
                        P.dma("sync", sot[:, :, 0:tc], SO[hh * 512:(hh + 1) * 512, ctok:ctok + tc].rearrange("(e p) t -> p e t", p=128), writes=[sob], sem_buf=sob)
                        P.op("act", lambda e: e.activation(out=hsq[:, :, 0:tc], in_=hf[:, :, 0:tc], func=AF.Square), reads=[hfb], writes=[hqb])

                    def s_ssq():
                        for ec in range(4):
                            P.op("pe", lambda e, ec=ec: e.matmul(misc[:, 0:tc], lhsT=ones_b[:], rhs=hsq[:, ec, 0:tc], start=(ec == 0), stop=(ec == 3)),
                                 reads=[hqb, CONST], writes=[mib], signal=(ec == 3))

                    def s_sqrt():
                        P.op("act", lambda e: e.activation(out=hrs[:, 0:tc], in_=misc[:, 0:tc], func=AF.Sqrt, bias=epst[:, 0:1], scale=1.0 / 512.0),
                             reads=[mib, CONST], writes=[hrb, mib])

                    def s_rstd():
                        P.op("dve", lambda e: e.reciprocal(out=hrs[:, 0:tc], in_=hrs[:, 0:tc]), reads=[hrb], writes=[hrb])

                    def s_norm():
                        for ec in range(4):
                            fch = hh * 4 + ec
                            P.op("dve", lambda e, ec=ec, fch=fch: e.scalar_tensor_tensor(out=hf[:, ec, 0:tc], in0=hf[:, ec, 0:tc], scalar=mhg[:, jl, fch:fch + 1], in1=hrs[:, 0:tc], op0=ALU.mult, op1=ALU.mult),
                                 reads=[hfb, hrb, CONST], writes=[hfb])

                    def s_gate():
                        sot, sob = st_["so"]
                        hot, hob = acquire(hor)
                        for ec in range(4):
                            P.op("pool", lambda e, ec=ec: e.tensor_tensor(out=hot[:, ec, 0:tc], in0=hf[:, ec, 0:tc], in1=sot[:, ec, 0:tc], op=ALU.mult),
                                 reads=[hfb, sob], writes=[hob])
                        P.dma("sync", HB[hh * 512:(hh + 1) * 512, ctok:ctok + tc].rearrange("(e p) t -> p e t", p=128), hot[:, :, 0:tc], reads=[hob], sem_buf=hob)
                    tail_q.extend([(s_add, (nsbb, hfb)), nop_, nop_, (s_sq, (hfb,)), (s_ssq, ()), (s_sqrt, ()), nop_, (s_rstd, ()), nop_, (s_norm, (hfb,)), (s_gate, (hfb,))])

                stageA(pairs[0])
                for i in range(len(pairs)):
                    if i + 1 < len(pairs):
                        stageA(pairs[i + 1])
                    stageBC(pairs[i])
                    if tail_q:
                        tail_q.pop(0)[0]()
        while tail_q:
            tail_q.pop(0)[0]()
        P.end_phase()

        pre2, epi2 = resid_epi(xs, xd, lambda j, g: MODv(l, 2, j, g), None)
        gemm(HB, KC, SB16L if last else SB16, [(j,) for j in range(KC)], lambda ch: m_wout[jl, ch], epi2, 1, pre=pre2, consumer=consumer, npb=4)

    epst = P.gtile("epst", [128, 1], F32)
    P.op("dve", lambda e: e.memset(epst[:], EPS), writes=[CONST])
    phase_mod()
    xcur = xin
    nxt = 0
    lys = list(layers if layers is not None else range(depth))
    for li, l in enumerate(lys):
        if li > 0:
            P.new_epoch()
        lastl = (li == len(lys) - 1)
        if li == 0:
            phase_norm(xcur, lambda kc, g, l=l: A1[:, l, kc, g:g + 1], lambda kc, g, l=l: MODv(l, 0, kc, g), HB, BF16)
        xmid = XR[nxt]
        c2 = (lambda l=l, xmid=xmid: NormConsumer(xmid, lambda kc, g: A2[:, l, kc, g:g + 1], lambda kc, g: MODv(l, 3, kc, g), HB2, BF16, 0, 512))
        if l % 2 == 0:
            phase_conv(l, xcur, xmid, last=lastl, consumer=c2)
        else:
            phase_mlstm(l, xcur, xmid, last=lastl, consumer=c2)
        nxt = 1 - nxt
        xnew = XR[nxt]
        if lastl:
            c1 = (lambda xnew=xnew: NormConsumer(xnew, lambda kc, g: fint[:, kc:kc + 1], lambda kc, g: None, out, F32, 512, 128))
        else:
            ln = lys[li + 1]
            c1 = (lambda xnew=xnew, ln=ln: NormConsumer(xnew, lambda kc, g: A1[:, ln, kc, g:g + 1], lambda kc, g: MODv(ln, 0, kc, g), HB, BF16, 0, 128))
        phase_ffn(l, xmid, xnew, last=lastl, consumer=c1)
        xcur = xnew
        nxt = 1 - nxt
    P.finish()
    return nc


def _chunk_layout(w):
    K, N = w.shape
    return np.ascontiguousarray(w.reshape(K // 128, 128, N // 128, 128).transpose(2, 1, 0, 3).reshape(N // 128, 128, K))


def _vec_layout(v):
    L, N = v.shape
    return np.ascontiguousarray(v.reshape(L, N // 128, 128).transpose(2, 0, 1))


def _consts():
    ident = np.eye(128, dtype=np.float32)
    masks = np.full((128, 8, 512), NEG, dtype=np.float32)
    s = np.arange(128)[:, None]
    c = np.arange(512)[None, :]
    for j in range(4):
        masks[:, j, :] = np.where(c >= j * 128 + s, 0.0, NEG)
        masks[:, 4 + j, :] = np.where(c <= j * 128 + s, 0.0, NEG)
    sel = np.zeros((4, 4, 128), dtype=np.float32)
    for h in range(4):
        sel[h, h, :] = 1.0
    return ident, masks, sel


def prep_shared(inp, depth=4):
    f = lambda a: np.asarray(a, dtype=np.float32)
    sh = {}
    sh["w_mod"] = np.stack([_chunk_layout(f(inp["w_mod"][l])) for l in range(4)])
    sh["b_mod"] = _vec_layout(f(inp["b_mod"]))
    sh["w_gu"] = np.stack([_chunk_layout(f(inp["w_gu"][l])) for l in range(4)])
    sh["w_down"] = np.stack([_chunk_layout(f(inp["w_down"][l])) for l in range(4)])
    sh["n1g"] = _vec_layout(f(inp["norm1_g"]))
    sh["n2g"] = _vec_layout(f(inp["norm2_g"]))
    sh["fing"] = _vec_layout(f(inp["final_g"])[None])[:, 0, :].copy()
    sh["c_pw1"] = np.stack([_chunk_layout(f(inp["conv_w_pw1"][j])) for j in range(2)])
    sh["c_bpw1"] = _vec_layout(f(inp["conv_b_pw1"]))
    wdw = f(inp["conv_w_dw"])
    sh["c_wdw"] = np.ascontiguousarray(wdw.reshape(2, 31, 16, 128).transpose(3, 0, 2, 1))
    sh["c_bdw"] = _vec_layout(f(inp["conv_b_dw"]))
    sh["c_lng"] = _vec_layout(f(inp["conv_ln_g"]))
    sh["c_lnb"] = _vec_layout(f(inp["conv_ln_b"]))
    sh["c_pw2"] = np.stack([_chunk_layout(f(inp["conv_w_pw2"][j])) for j in range(2)])
    sh["c_bpw2"] = _vec_layout(f(inp["conv_b_pw2"]))
    win = f(inp["m_w_in"])
    sh["m_win"] = np.stack([_chunk_layout(win[j][:, :6144]) for j in range(2)])
    wg = win[:, :, 6144:6160]
    sh["m_wg"] = np.ascontiguousarray(wg.reshape(2, 16, 128, 16).transpose(2, 0, 1, 3))
    bg = f(inp["m_b_gates"])
    sh["m_bg"] = np.ascontiguousarray(bg.reshape(2, 4, 4).transpose(2, 0, 1))
    sh["m_hng"] = _vec_layout(f(inp["m_hn_g"]))
    sh["m_wout"] = np.stack([_chunk_layout(f(inp["m_w_out"][j])) for j in range(2)])
    ident, masks, sel = _consts()
    sh["ident"] = ident
    sh["masks"] = masks
    sh["sel"] = sel
    return sh


def prep_core(inp, c):
    f = lambda a: np.asarray(a, dtype=np.float32)
    b0, b1 = 2 * c, 2 * c + 1
    x = inp["x"]
    ctx = inp["ctx"]
    xin = np.concatenate([f(ctx[b0]).T, f(ctx[b1]).T, f(x[b0]).T, f(x[b1]).T], axis=1)
    cv = np.stack([f(inp["c"][b0]), f(inp["c"][b1]), f(inp["c_ctx"])], axis=0)
    cT = np.ascontiguousarray(cv.reshape(3, 16, 128).transpose(2, 1, 0))
    return {"xin": np.ascontiguousarray(xin), "cT": cT}


_NC_CACHE = {}


def kernel(**inputs):
    n = 8
    if "nc" not in _NC_CACHE:
        _NC_CACHE["nc"] = build(4)
    nc = _NC_CACHE["nc"]
    sh = prep_shared(inputs)
    in_maps = []
    for c in range(n):
        m = dict(sh)
        m.update(prep_core(inputs, c))
        in_maps.append(m)
    res = run_bass_kernel_spmd(nc, in_maps, core_ids=list(range(n)))
    outs = []
    for c in range(n):
        o = np.asarray(res.results[c]["out"])
        outs.append(o[:, :2048].T)
        outs.append(o[:, 2048:].T)
    return np.ascontiguousarray(np.stack(outs, axis=0).astype(np.float32))
```
